# Optimizing a Trainium2 kernel written in Bass

```python
import math
import jax
import jax.numpy as jnp
from jax import lax
import numpy as np

D_MODEL = 1024
BATCH = 4
SEQ = 8192
DEPTH = 2
DEC_BATCH = 32
DEC_SEQ = 32
PAST_LEN = 4096

CHUNK = 64
BAND_CHUNKS = 8
ATT_WINDOW = BAND_CHUNKS * CHUNK
A_HEADS = 8
A_HEAD_DIM = D_MODEL // 16
A_WIDTH = A_HEADS * A_HEAD_DIM
REL_MAX = 128
N_REL = (CHUNK - 1) + REL_MAX + 1
B_HEADS = 8
B_HEAD_DIM = D_MODEL // 16
B_WIDTH = B_HEADS * B_HEAD_DIM
B_QKV = 3 * B_WIDTH
CONV_W = 4
GDN_CHUNK = 64
SPLIT_SIZES = (A_WIDTH, A_WIDTH, A_WIDTH, A_WIDTH, B_QKV, B_WIDTH, B_HEADS, B_HEADS, D_MODEL, D_MODEL)
IN_DIM = 4 * A_WIDTH + B_QKV + B_WIDTH + 2 * B_HEADS + 2 * D_MODEL
ALPHA = (2 * DEPTH) ** 0.25
BETA = (8 * DEPTH) ** -0.25
LN_EPS = 1e-5
NORM_EPS = 1e-6

kernel_name = 'hybrid_chunkband_gdn_streaming_step'


def layer_norm(x, g, b):
    xf = x.astype(jnp.float32)
    mu = jnp.mean(xf, axis=-1, keepdims=True)
    var = jnp.mean(jnp.square(xf - mu), axis=-1, keepdims=True)
    y = (xf - mu) * lax.rsqrt(var + LN_EPS) * g.astype(jnp.float32) + b.astype(jnp.float32)
    return y.astype(x.dtype)


def split_projection(p):
    cuts = np.cumsum(SPLIT_SIZES)[:-1].tolist()
    return jnp.split(p, cuts, axis=-1)


def l2_normalize(z):
    zf = z.astype(jnp.float32)
    return zf * lax.rsqrt(jnp.sum(zf * zf, axis=-1, keepdims=True) + NORM_EPS)


def band_attention(q, k, v, q_pos, k_pos, k_valid, rel_bias):
    s = jnp.einsum('bqhd,bkhd->bhqk', q, k).astype(jnp.float32) * (A_HEAD_DIM ** -0.5)
    rel = jnp.clip(q_pos[:, None] - k_pos[None, :], -(CHUNK - 1), REL_MAX) + (CHUNK - 1)
    s = s + rel_bias[:, rel].astype(jnp.float32)[None]
    s = jnp.where(k_valid[None, None, None, :], s, -1e30)
    p = jax.nn.softmax(s, axis=-1)
    return jnp.einsum('bhqk,bkhd->bqhd', p.astype(v.dtype), v)


def chunk_band_attention(q, k, v, rel_bias):
    bsz, t, h, d = q.shape
    n_chunks = t // CHUNK
    band = ATT_WINDOW + CHUNK
    pad = ((0, 0), (ATT_WINDOW, 0), (0, 0), (0, 0))
    kp = jnp.pad(k, pad)
    vp = jnp.pad(v, pad)
    q_blocks = jnp.moveaxis(q.reshape(bsz, n_chunks, CHUNK, h, d), 1, 0)
    k_offs = jnp.arange(band, dtype=jnp.int32)
    q_offs = jnp.arange(CHUNK, dtype=jnp.int32)

    def one_chunk(args):
        c, q_blk = args
        start = c * CHUNK
        k_blk = lax.dynamic_slice_in_dim(kp, start, band, axis=1)
        v_blk = lax.dynamic_slice_in_dim(vp, start, band, axis=1)
        k_pos = start - ATT_WINDOW + k_offs
        q_pos = start + q_offs
        return band_attention(q_blk, k_blk, v_blk, q_pos, k_pos, k_pos >= 0, rel_bias)

    o = lax.map(one_chunk, (jnp.arange(n_chunks, dtype=jnp.int32), q_blocks))
    return jnp.moveaxis(o, 0, 1).reshape(bsz, t, h, d)


def causal_conv(x_full, w, t):
    acc = x_full[:, 0:t] * w[0]
    for i in range(1, CONV_W):
        acc = acc + x_full[:, i:i + t] * w[i]
    return acc


def gated_delta_rule(q, k, v, g, beta, s0):
    f32 = jnp.float32
    bsz, t, h, dk = q.shape
    dv = v.shape[-1]
    blk = min(t, GDN_CHUNK)
    nb = t // blk

    def to_blocks(z):
        z = z.astype(f32).reshape((bsz, nb, blk, h) + z.shape[3:])
        return jnp.moveaxis(jnp.swapaxes(z, 2, 3), 1, 0)

    qb = to_blocks(q)
    kb = to_blocks(k)
    vb = to_blocks(v)
    gb = to_blocks(g)
    bb = to_blocks(beta)
    gc = jnp.cumsum(gb, axis=-1)
    causal = jnp.tril(jnp.ones((blk, blk), dtype=bool))
    strict = jnp.tril(jnp.ones((blk, blk), dtype=f32), -1)
    decay = jnp.exp(jnp.where(causal, gc[..., :, None] - gc[..., None, :], -jnp.inf))
    k_beta = kb * bb[..., None]
    v_beta = vb * bb[..., None]
    m = jnp.einsum('nbhid,nbhjd->nbhij', k_beta, kb) * decay * strict
    eye = jnp.eye(blk, dtype=f32)
    t_inv = lax.linalg.triangular_solve(eye + m, jnp.broadcast_to(eye, m.shape),
                                        left_side=True, lower=True, unit_diagonal=True)
    u = jnp.einsum('nbhij,nbhjd->nbhid', t_inv, v_beta)
    w = jnp.einsum('nbhij,nbhjd->nbhid', t_inv, k_beta * jnp.exp(gc)[..., None])
    qk = jnp.einsum('nbhid,nbhjd->nbhij', qb, kb) * decay

    def step(s, blk_in):
        q_c, k_c, u_c, w_c, gc_c, qk_c = blk_in
        v_new = u_c - jnp.einsum('bhld,bhde->bhle', w_c, s)
        o_c = (jnp.einsum('bhld,bhde->bhle', q_c * jnp.exp(gc_c)[..., None], s)
               + jnp.einsum('bhij,bhje->bhie', qk_c, v_new))
        g_last = gc_c[..., -1]
        k_dec = k_c * jnp.exp(g_last[..., None] - gc_c)[..., None]
        s = s * jnp.exp(g_last)[..., None, None] + jnp.einsum('bhld,bhle->bhde', k_dec, v_new)
        return s, o_c

    s_final, o = lax.scan(step, s0.astype(f32), (qb, kb, u, w, gc, qk))
    o = jnp.swapaxes(jnp.moveaxis(o, 0, 1), 2, 3).reshape(bsz, t, h, dv)
    return o, s_final


def trunk_layer(x, kv_past, conv_prev, s_prev, w_in, rel_bias, conv_w, a_log, dt_bias,
                gdn_norm_w, w_branch_a, w_branch_b, w_out, ln_g, ln_b):
    f32 = jnp.float32
    bsz, t, _ = x.shape
    qa, ka, va, za, qkv_b, zb, a_b, b_b, gate_a, gate_b = split_projection(x @ w_in)

    qa = qa.reshape(bsz, t, A_HEADS, A_HEAD_DIM)
    ka = ka.reshape(bsz, t, A_HEADS, A_HEAD_DIM)
    va = va.reshape(bsz, t, A_HEADS, A_HEAD_DIM)
    if kv_past is None:
        oa = chunk_band_attention(qa, ka, va, rel_bias)
        keep = min(ATT_WINDOW, t)
        new_k = ka[:, t - keep:]
        new_v = va[:, t - keep:]
    else:
        k_past, v_past = kv_past
        n_past = k_past.shape[1]
        k_all = jnp.concatenate([k_past.astype(ka.dtype), ka], axis=1)
        v_all = jnp.concatenate([v_past.astype(va.dtype), va], axis=1)
        q_pos = PAST_LEN + jnp.arange(t, dtype=jnp.int32)
        k_pos = PAST_LEN - n_past + jnp.arange(n_past + t, dtype=jnp.int32)
        oa = band_attention(qa, k_all, v_all, q_pos, k_pos, jnp.ones((n_past + t,), dtype=bool), rel_bias)
        new_k = ka
        new_v = va
    ya = (oa.reshape(bsz, t, A_WIDTH) * jax.nn.silu(za)) @ w_branch_a

    xp = jnp.concatenate([conv_prev.astype(qkv_b.dtype), qkv_b], axis=1)
    new_conv = xp[:, -(CONV_W - 1):]
    c = jax.nn.silu(causal_conv(xp, conv_w, t))
    qb, kb, vb = jnp.split(c, 3, axis=-1)
    qb = l2_normalize(qb.reshape(bsz, t, B_HEADS, B_HEAD_DIM)) * (B_HEAD_DIM ** -0.5)
    kb = l2_normalize(kb.reshape(bsz, t, B_HEADS, B_HEAD_DIM))
    vb = vb.reshape(bsz, t, B_HEADS, B_HEAD_DIM)
    g = -jnp.exp(a_log.astype(f32)) * jax.nn.softplus(a_b.astype(f32) + dt_bias.astype(f32))
    beta = jax.nn.sigmoid(b_b.astype(f32))
    ob, s_new = gated_delta_rule(qb, kb, vb, g, beta, s_prev)
    ob = ob * lax.rsqrt(jnp.mean(ob * ob, axis=-1, keepdims=True) + NORM_EPS) * gdn_norm_w.astype(f32)
    ob = ob.astype(x.dtype).reshape(bsz, t, B_WIDTH)
    yb = (ob * jax.nn.silu(zb)) @ w_branch_b

    mix = jax.nn.sigmoid(gate_a) * ya + jax.nn.sigmoid(gate_b) * yb
    y = layer_norm(ALPHA * x + mix @ w_out, ln_g, ln_b)
    return y, (new_k, new_v, new_conv, s_new.astype(x.dtype))


def setup_inputs(seed: int = 0) -> dict:
    key = jax.random.key(seed)
    ks = jax.random.split(key, 24)
    nrm = jax.random.normal
    n_cache = min(ATT_WINDOW, PAST_LEN)
    x_prompt = nrm(ks[0], (BATCH, SEQ, D_MODEL), jnp.float32)
    x_sample = nrm(ks[1], (DEC_BATCH, DEC_SEQ, D_MODEL), jnp.float32)
    cache_attn_k = nrm(ks[2], (DEPTH, DEC_BATCH, n_cache, A_HEADS, A_HEAD_DIM), jnp.float32)
    cache_attn_v = nrm(ks[3], (DEPTH, DEC_BATCH, n_cache, A_HEADS, A_HEAD_DIM), jnp.float32) * BETA
    state_conv = nrm(ks[4], (DEPTH, DEC_BATCH, CONV_W - 1, B_QKV), jnp.float32)
    state_gdn = nrm(ks[5], (DEPTH, DEC_BATCH, B_HEADS, B_HEAD_DIM, B_HEAD_DIM), jnp.float32) * 0.05
    ln0_g = 1.0 + 0.05 * nrm(ks[6], (D_MODEL,), jnp.float32)
    ln0_b = 0.02 * nrm(ks[7], (D_MODEL,), jnp.float32)
    col_scale = np.ones((IN_DIM,), np.float32)
    col_scale[2 * A_WIDTH:3 * A_WIDTH] = BETA
    v_off = 4 * A_WIDTH + 2 * B_WIDTH
    col_scale[v_off:v_off + B_WIDTH] = BETA
    w_in = nrm(ks[8], (DEPTH, D_MODEL, IN_DIM), jnp.float32) * (D_MODEL ** -0.5) * jnp.asarray(col_scale)
    rel_bias = 0.5 * nrm(ks[9], (DEPTH, A_HEADS, N_REL), jnp.float32)
    conv_w = nrm(ks[10], (DEPTH, CONV_W, B_QKV), jnp.float32) * (CONV_W ** -0.5)
    a_log = jnp.log(jax.random.uniform(ks[11], (DEPTH, B_HEADS), jnp.float32, 1.0, 16.0))
    dt = jnp.exp(jax.random.uniform(ks[12], (DEPTH, B_HEADS), jnp.float32, math.log(1e-3), math.log(1e-1)))
    dt_bias = dt + jnp.log(-jnp.expm1(-dt))
    gdn_norm_w = 1.0 + 0.05 * nrm(ks[13], (DEPTH, B_HEAD_DIM), jnp.float32)
    w_branch_a = nrm(ks[14], (DEPTH, A_WIDTH, D_MODEL), jnp.float32) * (A_WIDTH ** -0.5) * BETA
    w_branch_b = nrm(ks[15], (DEPTH, B_WIDTH, D_MODEL), jnp.float32) * (B_WIDTH ** -0.5) * BETA
    w_out = nrm(ks[16], (DEPTH, D_MODEL, D_MODEL), jnp.float32) * (D_MODEL ** -0.5) * BETA
    ln_g = 1.0 + 0.05 * nrm(ks[17], (DEPTH, D_MODEL), jnp.float32)
    ln_b = 0.02 * nrm(ks[18], (DEPTH, D_MODEL), jnp.float32)
    return {'x_prompt': x_prompt, 'x_sample': x_sample,
            'cache_attn_k': cache_attn_k, 'cache_attn_v': cache_attn_v,
            'state_conv': state_conv, 'state_gdn': state_gdn,
            'ln0_g': ln0_g, 'ln0_b': ln0_b, 'w_in': w_in, 'rel_bias': rel_bias,
            'conv_w': conv_w, 'a_log': a_log, 'dt_bias': dt_bias, 'gdn_norm_w': gdn_norm_w,
            'w_branch_a': w_branch_a, 'w_branch_b': w_branch_b, 'w_out': w_out,
            'ln_g': ln_g, 'ln_b': ln_b}


def reference(x_prompt, x_sample, cache_attn_k, cache_attn_v, state_conv, state_gdn,
              ln0_g, ln0_b, w_in, rel_bias, conv_w, a_log, dt_bias, gdn_norm_w,
              w_branch_a, w_branch_b, w_out, ln_g, ln_b):
    hp = layer_norm(x_prompt, ln0_g, ln0_b)
    hs = layer_norm(x_sample, ln0_g, ln0_b)
    n_prompt = x_prompt.shape[0]
    conv_zero = jnp.zeros((n_prompt, CONV_W - 1, B_QKV), x_prompt.dtype)
    s_zero = jnp.zeros((n_prompt, B_HEADS, B_HEAD_DIM, B_HEAD_DIM), jnp.float32)
    kp_list = []
    vp_list = []
    cp_list = []
    sp_list = []
    ks_list = []
    vs_list = []
    cs_list = []
    ss_list = []
    for l in range(DEPTH):
        params = (w_in[l], rel_bias[l], conv_w[l], a_log[l], dt_bias[l], gdn_norm_w[l],
                  w_branch_a[l], w_branch_b[l], w_out[l], ln_g[l], ln_b[l])
        hp, (k_p, v_p, c_p, s_p) = trunk_layer(hp, None, conv_zero, s_zero, *params)
        hs, (k_s, v_s, c_s, s_s) = trunk_layer(hs, (cache_attn_k[l], cache_attn_v[l]),
                                               state_conv[l], state_gdn[l], *params)
        kp_list.append(k_p)
        vp_list.append(v_p)
        cp_list.append(c_p)
        sp_list.append(s_p)
        ks_list.append(k_s)
        vs_list.append(v_s)
        cs_list.append(c_s)
        ss_list.append(s_s)
    return (hp, hs,
            jnp.stack(kp_list), jnp.stack(vp_list), jnp.stack(cp_list), jnp.stack(sp_list),
            jnp.stack(ks_list), jnp.stack(vs_list), jnp.stack(cs_list), jnp.stack(ss_list))
```

```python
import contextlib
import numpy as np
import concourse.bass as bass
import concourse.mybir as mybir
from concourse.bass_utils import run_bass_kernel_spmd

F32 = mybir.dt.float32
BF16 = mybir.dt.bfloat16
AF = mybir.ActivationFunctionType
ALU = mybir.AluOpType
AX = mybir.AxisListType

D = 1024
IN_DIM = 6160
ALPHA = 4.0 ** 0.25
LN_EPS = 1e-5
NORM_EPS = 1e-6
BIG = 30000.0
C_QA, C_KA, C_VA, C_ZA, C_QKVB, C_ZB, C_AB, C_GA, C_GB = 0, 512, 1024, 1536, 2048, 3584, 4096, 4112, 5136

CFG = dict(T=8192, NSB=4, DEPTH=2, NCORES=8)


class Sched:
    ENGS = ['pe', 'act', 'dve', 'pool', 'sp']
    KDMA = 8
    EPOCH = 4000
    NEP = 12

    def __init__(self):
        self.ops = {e: [] for e in self.ENGS}
        self.res = {}

    def op(self, eng, fn, r=(), w=(), dma=False):
        idx = len(self.ops[eng])
        deps = set()
        rl = _flat(r)
        wl = _flat(w)
        for n in rl:
            st = self.res.setdefault(n, [None, {}, []])
            if st[0] is not None:
                deps.add(st[0])
            if n.startswith('ps'):
                for e_, i_ in st[1].items():
                    if e_ != eng:
                        deps.add((e_, i_))
        for n in wl:
            st = self.res.setdefault(n, [None, {}, []])
            if st[0] is not None:
                deps.add(st[0])
            for e_, i_ in st[1].items():
                deps.add((e_, i_))
            for rd in st[2]:
                deps.add(rd)
        for n in rl:
            if dma:
                self.res[n][2].append((eng, idx))
            else:
                self.res[n][1][eng] = idx
        for n in wl:
            self.res[n] = [(eng, idx), {}, []]
        deps.discard((eng, idx))
        self.ops[eng].append(dict(fn=fn, deps=deps, dma=dma, sig=False))

    def finalize(self):
        for e in self.ENGS:
            for o in self.ops[e]:
                for (d, i) in o['deps']:
                    if d == 'pe' and e == 'pe':
                        continue
                    self.ops[d][i]['sig'] = True
        for e in self.ENGS:
            cnt = 0
            nd = 0
            for o in self.ops[e]:
                if o['dma']:
                    o['dsem'] = nd % self.KDMA
                    o['dval'] = 16 * (nd // self.KDMA + 1)
                    nd += 1
                elif o['sig']:
                    o['ep'] = cnt // self.EPOCH
                    o['sval'] = cnt % self.EPOCH + 1
                    cnt += 1
            assert cnt <= self.EPOCH * self.NEP, (e, cnt)

    def emit(self, nc, stack):
        self.finalize()
        sems = {e: [stack.enter_context(nc.semaphore(f"s_{e}_{k}")) for k in range(self.NEP)] for e in self.ENGS}
        dsems = {e: [stack.enter_context(nc.semaphore(f"d_{e}_{k}")) for k in range(self.KDMA)] for e in self.ENGS}
        block = stack.enter_context(nc.Block())
        ops = self.ops
        KD = self.KDMA

        def body(ename):
            def run(eng):
                waited = {}

                def wait(key, sem, val):
                    if waited.get(key, 0) >= val:
                        return
                    waited[key] = val
                    eng.wait_ge(sem, val)
                dcount = {}
                for o in ops[ename]:
                    for (d, i) in sorted(o['deps']):
                        po = ops[d][i]
                        if po['dma']:
                            wait(('d', d, po['dsem']), dsems[d][po['dsem']], po['dval'])
                        else:
                            if d == 'pe' and ename == 'pe':
                                continue
                            wait(('s', d, po['ep']), sems[d][po['ep']], po['sval'])
                    if o['dma']:
                        if o['dval'] > 16:
                            wait(('d', ename, o['dsem']), dsems[ename][o['dsem']], o['dval'] - 16)
                        inst = o['fn'](eng)
                        inst.then_inc(dsems[ename][o['dsem']], 16)
                        dcount[o['dsem']] = o['dval']
                    else:
                        inst = o['fn'](eng)
                        if o['sig']:
                            inst.then_inc(sems[ename][o['ep']], 1)
                for k, v in sorted(dcount.items()):
                    wait(('d', ename, k), dsems[ename][k], v)
            return run

        block.tensor(body('pe'))
        block.scalar(body('act'))
        block.vector(body('dve'))
        block.gpsimd(body('pool'))
        block.sync(body('sp'))


def _flat(x):
    out = []
    for a in x:
        if isinstance(a, (list, tuple)):
            out.extend(_flat(a))
        else:
            out.append(a)
    return out


def host_consts():
    i = np.arange(128)[:, None]
    j = np.arange(128)[None, :]
    same = (i // 64) == (j // 64)
    c = np.zeros((12, 128, 128), np.float32)
    c[0] = np.eye(128)
    c[1] = 1.0
    c[2] = (same & (i <= j))
    c[3] = np.where(same & (j <= i), 0.0, BIG)
    c[4] = np.where(same & (j < i), -1.0, 0.0)
    c[5] = same
    c[6] = np.where(i >= 32, -BIG, 0.0) * np.ones((1, 128))
    c[7][:, 0] = (np.arange(128) < 32)
    c[7][:, 1] = 1.0
    c[8] = np.where((i < 64) & (j >= 64), -BIG, 0.0)
    c[9] = np.where((i >= 64) & (j < 64), -BIG, 0.0)
    c[10] = np.where(same & (j <= i), 1.0, 0.0)
    return np.ascontiguousarray(c.transpose(1, 0, 2).reshape(128, 12 * 128))


def bias_index():
    kk = np.arange(128)[:, None]
    qq = np.arange(128)[None, :]
    out = []
    for p in (0, 3, 4):
        out.append(np.clip(qq - kk + (4 - p) * 128, -63, 128) + 63)
    return out


def build(cfg):
    T, NSB, DEPTH = cfg['T'], cfg['NSB'], cfg['DEPTH']
    NT = T // 128
    KEEP = min(512, T)
    NKT = KEEP // 128
    nc = bass.Bass("TRN2", target_bir_lowering=False)
    S = Sched()
    es = contextlib.ExitStack()

    def din(name, shape):
        return nc.dram_tensor(name, list(shape), F32, kind="ExternalInput").ap()

    def dout(name, shape):
        return nc.dram_tensor(name, list(shape), F32, kind="ExternalOutput").ap()

    def dint(name, shape):
        return nc.dram_tensor(name, list(shape), F32, kind="Internal").ap()

    xp = din("xp", [T, D])
    xs = din("xs", [NSB * 128, D])
    ck = din("ck", [DEPTH, NSB, 512, 512])
    cv = din("cv", [DEPTH, NSB, 512, 512])
    sconv = din("sconv", [DEPTH, NSB, 3, 1536])
    sgdn = din("sgdn", [DEPTH, NSB, 8, 64, 64])
    w_in = din("w_in", [DEPTH, D, IN_DIM])
    w_a = din("w_a", [DEPTH, 512, D])
    w_b = din("w_b", [DEPTH, 512, D])
    w_o = din("w_o", [DEPTH, D, D])
    convw = din("convw", [DEPTH, 1536, 4])
    ln0 = din("ln0", [2, 128, D])
    lnl = din("lnl", [DEPTH, 2, 128, D])
    alog = din("alog", [DEPTH, 128, 8])
    dtb = din("dtb", [DEPTH, 128, 8])
    gnw = din("gnw", [DEPTH, 128, 64])
    bt = din("bt", [DEPTH, 8, 3, 128, 128])
    crow = din("crow", [DEPTH, 128, 8])
    consts = din("consts", [128, 12 * 128])

    yp = dout("yp", [T, D])
    ys = dout("ys", [NSB * 128, D])
    nkp = dout("nkp", [DEPTH, KEEP, 512])
    nvp = dout("nvp", [DEPTH, KEEP, 512])
    ncp = dout("ncp", [DEPTH, 3, 1536])
    ngp = dout("ngp", [DEPTH, 8, 64, 64])
    nks = dout("nks", [DEPTH, NSB, 32, 512])
    nvs = dout("nvs", [DEPTH, NSB, 32, 512])
    ncs = dout("ncs", [DEPTH, NSB, 3, 1536])
    ngs = dout("ngs", [DEPTH, NSB, 8, 64, 64])
    hbuf_p = [dint("h0p", [T, D]), dint("h1p", [T, D])]
    hbuf_s = [dint("h0s", [NSB * 128, D]), dint("h1s", [NSB * 128, D])]

    def sb(name, shape, dt=F32):
        return es.enter_context(nc.sbuf_tensor(name, list(shape), dt))

    Win = sb("Win", [128, 8, IN_DIM], BF16)
    Wa = sb("Wa", [128, 4, D], BF16)
    Wb = sb("Wb", [128, 4, D], BF16)
    Wo = sb("Wo", [128, 8, D], BF16)
    CF = sb("CF", [128, 6, 128])
    identb = sb("identb", [128, 128], BF16)
    bdonesb = sb("bdonesb", [128, 128], BF16)
    lnG = sb("lnG", [128, D])
    lnB = sb("lnB", [128, D])
    hx = [sb("hx0", [128, D]), sb("hx1", [128, D])]
    hTs = [sb("hT0", [128, 8, 128], BF16), sb("hT1", [128, 8, 128], BF16)]
    qT = sb("qT", [128, 4, 128], BF16)
    kring = sb("kring", [128, 4, 640], BF16)
    vring = sb("vring", [128, 5, 8, 65], BF16)
    BT = sb("BT", [128, 8, 2, 128], BF16)
    maskp0b = sb("maskp0b", [128, 128], BF16)
    maskSb = sb("maskSb", [128, 128], BF16)
    MBb = sb("MBb", [128, 128], BF16)
    crows = sb("crows", [128, 8])
    za_s = sb("za_s", [128, 512], BF16)
    zb_s = sb("zb_s", [128, 512], BF16)
    xb = sb("xb", [128, 12, 131])
    cw = sb("cw", [128, 12, 4])
    Sbd = sb("Sbd", [128, 4, 128])
    Sb = sb("Sb", [128, 4, 128], BF16)
    oagT = sb("oagT", [128, 4, 128], BF16)
    obgT = sb("obgT", [128, 4, 128], BF16)
    small = sb("small", [128, 24, 8])
    negA = sb("negA", [128, 8])
    dtbs = sb("dtbs", [128, 8])
    gnws = sb("gnws", [128, 64])
    ARW = 6400
    arena = sb("arena", [128, ARW])
    arena_b = arena.bitcast(BF16)

    class Arena:
        def __init__(self):
            self.off = 0

        def reset(self):
            self.off = 0
            self.topoff = ARW

        def mark(self):
            return self.off

        def top(self, words):
            self.topoff = getattr(self, 'topoff', ARW) - words
            o = self.topoff
            res = [f"ar{g}" for g in range(o // 256, (o + words - 1) // 256 + 1)]
            return arena[:, o:o + words], res

        def release(self, m):
            self.off = m

        def get(self, words, dt=F32, shape=None):
            o = self.off
            self.off += words
            assert self.off <= ARW, self.off
            res = [f"ar{g}" for g in range(o // 256, (o + words - 1) // 256 + 1)]
            if dt == F32:
                ap = arena[:, o:o + words]
            else:
                ap = arena_b[:, 2 * o:2 * o + 2 * words]
            if shape is not None and len(shape) == 2:
                ap = ap.rearrange("p (a b) -> p a b", a=shape[0])
            return ap, res
    AR = Arena()

    PS = es.enter_context(nc.psum_tensor("PS", [128, 4096], F32))
    ps = [PS[:, i * 512:(i + 1) * 512] for i in range(8)]
    PSN = [f"ps{i}" for i in range(8)]

    def cf(i):
        return CF[:, i, :]

    def dma(out, in_, r, w, eng='sp', slow=False):
        if slow:
            S.op(eng, lambda e: e.dma_start(out=out, in_=in_, allow_slow_non_contiguous=True), r=r, w=w, dma=True)
        else:
            S.op(eng, lambda e: e.dma_start(out=out, in_=in_), r=r, w=w, dma=True)

    def mm(out, lhsT, rhs, start, stop, r, w, tp=None):
        if tp is None:
            S.op('pe', lambda e: e.matmul(out, lhsT, rhs, start=start, stop=stop), r=r, w=w)
        else:
            S.op('pe', lambda e: e.matmul(out, lhsT, rhs, start=start, stop=stop, tile_position=tp), r=r, w=w)

    def tr(out, in_, ident, r, w):
        S.op('pe', lambda e: e.transpose(out, in_, ident), r=r, w=w)

    def act(out, in_, func, r, w, bias=None, scale=None, accum=None):
        kw = {}
        if bias is not None:
            kw['bias'] = bias
        if scale is not None:
            kw['scale'] = scale
        if accum is not None:
            kw['accum_out'] = accum
        S.op('act', lambda e: e.activation(out, in_, func, **kw), r=r, w=w)

    def ts(eng, out, in0, s1, s2, op0, op1, r, w):
        if op1 is None:
            S.op(eng, lambda e: e.tensor_scalar(out, in0, s1, None, op0), r=r, w=w)
        else:
            S.op(eng, lambda e: e.tensor_scalar(out, in0, s1, s2, op0, op1), r=r, w=w)

    def tt(eng, out, in0, in1, op, r, w):
        S.op(eng, lambda e: e.tensor_tensor(out, in0, in1, op), r=r, w=w)

    def stt(eng, out, in0, sc, in1, op0, op1, r, w):
        S.op(eng, lambda e: e.scalar_tensor_tensor(out, in0, sc, in1, op0, op1), r=r, w=w)

    def cp(eng, out, in_, r, w):
        if eng == 'act':
            S.op('act', lambda e: e.activation(out, in_, AF.Copy), r=r, w=w)
        else:
            S.op(eng, lambda e: e.tensor_copy(out, in_), r=r, w=w)

    c3 = consts.rearrange("p (a b) -> p a b", a=12)
    dma(CF[:, 0:5, :], c3[:, 0:5, :], r=[], w=["CF"])
    dma(CF[:, 5, :], c3[:, 7, :], r=[], w=["CF"])
    VALID_S = CF[:, 5, 0:1]
    VALID_P = CF[:, 5, 1:2]
    AR.reset()
    stg0, rs0 = AR.get(512, F32, (4, 128))
    dma(stg0[:, 0, :], c3[:, 5, :], r=[], w=rs0)
    dma(stg0[:, 1, :], c3[:, 6, :], r=[], w=rs0)
    dma(stg0[:, 2, :], c3[:, 8, :], r=[], w=rs0)
    dma(stg0[:, 3, :], c3[:, 9, :], r=[], w=rs0)
    maskp4f = sb("maskp4f", [128, 128])
    cp('dve', identb[:], cf(0), r=["CF"], w=["identb"])
    cp('dve', MBb[:], cf(3), r=["CF"], w=["MBb"])
    cp('dve', bdonesb[:], stg0[:, 0, :], r=rs0, w=["bdonesb"])
    cp('dve', maskSb[:], stg0[:, 1, :], r=rs0, w=["maskSb"])
    cp('dve', maskp0b[:], stg0[:, 2, :], r=rs0, w=["maskp0b"])
    cp('dve', maskp4f[:], stg0[:, 3, :], r=rs0, w=["maskp4f"])
    S.op('dve', lambda e: e.memset(vring[:], 1.0), r=[], w=["vring"])

    def layer_norm_rows(src, dst, rs, rd):
        st = small[:, 16:22, :].rearrange("p a b -> p (a b)")
        stats = st[:, 0:12]
        mv = st[:, 12:14]
        rstd = st[:, 14:15]
        xr = src.rearrange("p (c f) -> p c f", c=2)
        S.op('dve', lambda e: e.bn_stats(stats[:, 0:6], xr[:, 0, :]), r=[rs], w=["lnst"])
        S.op('dve', lambda e: e.bn_stats(stats[:, 6:12], xr[:, 1, :]), r=[rs], w=["lnst"])
        S.op('dve', lambda e: e.bn_aggr(mv, stats.rearrange("p (c s) -> p c s", c=2)), r=["lnst"], w=["lnst"])
        act(rstd, mv[:, 1:2], AF.Sqrt, r=["lnst"], w=["lnst"], bias=LN_EPS, scale=1.0)
        S.op('dve', lambda e: e.reciprocal(rstd, rstd), r=["lnst"], w=["lnst"])
        stt('dve', dst, src, mv[:, 0:1], lnG[:], ALU.subtract, ALU.mult, r=[rs, "lnst", "lnG"], w=[rd])
        stt('dve', dst, dst, rstd, lnB[:], ALU.mult, ALU.add, r=[rd, "lnst", "lnB"], w=[rd])

    dma(lnG[:], ln0[0], r=[], w=["lnG"])
    dma(lnB[:], ln0[1], r=[], w=["lnB"])
    pbufs = [AR.get(1024) for _ in range(4)]
    tiles = [(xp, hbuf_p[0], t, "h0p") for t in range(NT)] + [(xs, hbuf_s[0], t, "h0s") for t in range(NSB)]
    for i, (srcd, dstd, t, rn) in enumerate(tiles):
        b, rb = pbufs[i % 4]
        dma(b, srcd[t * 128:(t + 1) * 128, :], r=[], w=rb)
        layer_norm_rows(b, b, rb, rb)
        dma(dstd[t * 128:(t + 1) * 128, :], b, r=rb, w=[f"{rn}_{t}"], eng='sp')

    cast_rr = [0]

    def cast(out, in_, r, w):
        k = cast_rr[0] % 3
        cast_rr[0] += 1
        cp(['pool', 'act', 'dve'][k], out, in_, r, w)

    def load_layer(l):
        AR.reset()
        stg = [AR.get(1540), AR.get(1540)]
        n = 0
        for kc in range(8):
            for cb in range(4):
                s_, r_ = stg[n % 2]
                n += 1
                dma(s_, w_in[l, kc * 128:(kc + 1) * 128, cb * 1540:(cb + 1) * 1540], r=[], w=r_)
                cast(Win[:, kc, cb * 1540:(cb + 1) * 1540], s_, r=r_, w=["Win"])
        for (wsrc, wdst, wn, nk) in ((w_a, Wa, "Wa", 4), (w_b, Wb, "Wb", 4), (w_o, Wo, "Wo", 8)):
            for kc in range(nk):
                s_, r_ = stg[n % 2]
                n += 1
                dma(s_[:, 0:1024], wsrc[l, kc * 128:(kc + 1) * 128, :], r=[], w=r_)
                cast(wdst[:, kc, :], s_[:, 0:1024], r=r_, w=[wn])
        dma(lnG[:], lnl[l, 0], r=[], w=["lnG"])
        dma(lnB[:], lnl[l, 1], r=[], w=["lnB"])
        dma(cw[:], convw[l].rearrange("(c p) i -> p c i", p=128), r=[], w=["cw"])
        dma(dtbs[:], dtb[l], r=[], w=["dtbs"])
        dma(gnws[:], gnw[l], r=[], w=["gnws"])
        dma(crows[:], crow[l], r=[], w=["crows"])
        dma(negA[:], alog[l], r=[], w=["negA"])
        act(negA[:], negA[:], AF.Exp, r=["negA"], w=["negA"])
        ts('dve', negA[:], negA[:], -1.0, None, ALU.mult, None, r=["negA"], w=["negA"])
        for h in range(8):
            for pi in (1, 2):
                s_, r_ = stg[n % 2]
                n += 1
                dma(s_[:, 0:128], bt[l, h, pi], r=[], w=r_)
                if pi == 1:
                    ts('dve', BT[:, h, 0, :], s_[:, 0:128], crows[:, h:h + 1], None, ALU.subtract, None, r=[r_, "crows"], w=["BT"])
                else:
                    stt('dve', BT[:, h, 1, :], s_[:, 0:128], crows[:, h:h + 1], maskp4f[:], ALU.subtract, ALU.add,
                        r=[r_, "crows", "maskp4f"], w=["BT"])

    def issue_load(l, kind, t, tix):
        hin_ = (hbuf_s if kind == 's' else hbuf_p)[l]
        dma(hx[tix % 2][:], hin_[t * 128:(t + 1) * 128, :], r=[f"h{l}{kind}_{t}"], w=[f"hx{tix % 2}"])

    def ac_stage(l, kind, t, tix):
        sample = (kind == 's')
        h = hx[tix % 2]
        hn = f"hx{tix % 2}"
        hT = hTs[tix % 2]
        hTn = f"hT{tix % 2}"
        first = sample or t == 0
        slot = 4 if sample else t % 5
        v4 = lambda ap: ap.rearrange("p (a b) -> p a b", a=4)
        v8 = lambda ap: ap.rearrange("p (h d) -> p h d", h=8)
        for half in range(2):
            for c in range(4):
                kc = half * 4 + c
                tr(ps[half][:, c * 128:(c + 1) * 128], h[:, kc * 128:(kc + 1) * 128], cf(0), r=[hn, "CF"], w=[PSN[half]])
            cp('act' if half else 'dve', hT[:, half * 4:half * 4 + 4, :], v4(ps[half][:]), r=[PSN[half]], w=[hTn])
            yield

        def proj_fm(col0, nch, bank):
            for c in range(nch):
                for kc in range(8):
                    mm(ps[bank][:, c * 128:(c + 1) * 128], Win[:, kc, col0 + c * 128:col0 + (c + 1) * 128], hT[:, kc, :],
                       kc == 0, kc == 7, r=["Win", hTn], w=[PSN[bank]])
                yield

        def proj_tm(col0, ncol, bank):
            for kc in range(8):
                mm(ps[bank][:, 0:ncol], hT[:, kc, :], Win[:, kc, col0:col0 + ncol], kc == 0, kc == 7,
                   r=["Win", hTn], w=[PSN[bank]])
                if kc == 3:
                    yield
            yield

        if sample:
            AR.reset()
            sbufs = [AR.get(512) for _ in range(4)]
            for blk in range(4):
                stg, rstg = sbufs[(2 * blk) % 4]
                dma(stg, ck[l, t, blk * 128:(blk + 1) * 128, :], r=[], w=rstg)
                for c in range(4):
                    tr(ps[2][:, c * 128:(c + 1) * 128], stg[:, c * 128:(c + 1) * 128], cf(0), r=[rstg, "CF"], w=[PSN[2]])
                cp('dve', kring[:, :, blk * 128:(blk + 1) * 128], v4(ps[2][:]), r=[PSN[2]], w=["kring"])
                stg2, rstg2 = sbufs[(2 * blk + 1) % 4]
                dma(stg2, cv[l, t, blk * 128:(blk + 1) * 128, :], r=[], w=rstg2)
                cp('act', vring[:, blk, :, 0:64], v8(stg2), r=rstg2, w=["vring"])
            AR.reset()
            sct, rsct = AR.get(1536)
            dma(sct[0:3, :], sconv[l, t], r=[], w=rsct)
            for c in range(12):
                tr(ps[3][:, c * 3:(c + 1) * 3], sct[0:3, c * 128:(c + 1) * 128], CF[0:3, 0, 0:3], r=[rsct, "CF"], w=[PSN[3]])
            cp('dve', xb[:, :, 128:131], ps[3][:, 0:36].rearrange("p (c i) -> p c i", c=12), r=[PSN[3]], w=["xb"])
            AR.reset()
            S.op('dve', lambda e: e.memset(Sbd[:], 0.0), r=[], w=["Sbd"])
            for pr in range(4):
                for par in range(2):
                    dma(Sbd[par * 64:(par + 1) * 64, pr, par * 64:(par + 1) * 64], sgdn[l, t, pr * 2 + par], r=[], w=["Sbd"])
            cp('act', Sb[:], Sbd[:], r=["Sbd"], w=["Sb"])
        elif first:
            S.op('dve', lambda e: e.memset(xb[:, :, 128:131], 0.0), r=[], w=["xb"])
            S.op('dve', lambda e: e.memset(Sbd[:], 0.0), r=[], w=["Sbd"])
            S.op('dve', lambda e: e.memset(Sb[:], 0.0), r=[], w=["Sb"])

        yield from proj_fm(C_QA, 4, 0)
        ts('dve', qT[:], v4(ps[0][:]), 0.125, None, ALU.mult, None, r=[PSN[0]], w=["qT"])
        yield from proj_fm(C_KA, 4, 1)
        cp('act', kring[:, :, slot * 128:(slot + 1) * 128], v4(ps[1][:]), r=[PSN[1]], w=["kring"])
        yield from proj_tm(C_VA, 512, 0)
        cp('dve', vring[:, slot, :, 0:64], v8(ps[0][:]), r=[PSN[0]], w=["vring"])
        want_kv = sample or (t >= NT - NKT)
        if want_kv:
            AR.reset()
            vst, rv = AR.get(512)
            cp('act', vst, ps[0][:], r=[PSN[0]], w=rv)
            if sample:
                dma(nvs[l, t], vst[0:32, :], r=rv, w=["nvs"], eng='sp')
            else:
                o0 = (t - (NT - NKT)) * 128
                dma(nvp[l, o0:o0 + 128, :], vst, r=rv, w=["nvp"], eng='sp')
        yield from proj_tm(C_ZA, 512, 1)
        act(za_s[:], ps[1][:], AF.Silu, r=[PSN[1]], w=["za_s"])
        yield
        if want_kv:
            yield from proj_tm(C_KA, 512, 0)
            kst, rk = AR.get(512)
            cp('act', kst, ps[0][:], r=[PSN[0]], w=rk)
            if sample:
                dma(nks[l, t], kst[0:32, :], r=rk, w=["nks"], eng='sp')
            else:
                o0 = (t - (NT - NKT)) * 128
                dma(nkp[l, o0:o0 + 128, :], kst, r=rk, w=["nkp"], eng='sp')
            yield

    def tile(l, kind, t, tix, ac_done=False, prefetch=None, next_ac=None):
        sample = (kind == 's')
        last_layer = (l == DEPTH - 1)
        if last_layer:
            hout = ys if sample else yp
            hout_name = f"y{kind}_{t}"
        else:
            hout = (hbuf_s if sample else hbuf_p)[l + 1]
            hout_name = f"h{l + 1}{kind}_{t}"
        h = hx[tix % 2]
        hn = f"hx{tix % 2}"
        hT = hTs[tix % 2]
        hTn = f"hT{tix % 2}"
        validcol = VALID_S if sample else VALID_P
        SMALL = "small"
        sm = lambda i: small[:, i, :]
        v4 = lambda ap: ap.rearrange("p (a b) -> p a b", a=4)
        v8 = lambda ap: ap.rearrange("p (h d) -> p h d", h=8)

        def proj_tm(col0, ncol, bank):
            for kc in range(8):
                mm(ps[bank][:, 0:ncol], hT[:, kc, :], Win[:, kc, col0:col0 + ncol], kc == 0, kc == 7,
                   r=["Win", hTn], w=[PSN[bank]])

        if not ac_done:
            for _ in ac_stage(l, kind, t, tix):
                pass
        if prefetch is not None:
            prefetch()
        AR.reset()

        AR.reset()
        qh, rqh = AR.get(256, BF16, (4, 128))
        kh, rkh = AR.get(256, BF16, (4, 128))
        kdec, rkd = AR.get(256, BF16)
        vbeta, rvb = AR.get(512)
        m_long = AR.mark()
        PTs = [AR.get(320, BF16), AR.get(320, BF16)]
        oag, roag = AR.get(512)
        rden, rrden = AR.get(8)
        cs, rcs = AR.get(1536, F32, (12, 128))
        rcs_qk, rcs_v = rcs[:4], rcs[4:]
        ctmp, rct = AR.get(1024, F32, (8, 128))
        AR.off -= 1024
        sq, rsq = AR.get(512, BF16, (8, 128))
        AR.off -= 512
        rinv, rri = AR.get(1024, F32, (8, 128))
        ctv, rctv = AR.get(512, F32, (4, 128))
        khf, rkhf = AR.get(512, F32, (4, 128))
        if sample:
            plist = [0, 1, 2, 3, 4]
        else:
            plist = [p for p in range(5) if t - 4 + p >= 0]

        def g_attn():
            for hh in range(8):
                pr, par = hh // 2, hh % 2
                bA, bB = (4, 5) if par == 0 else (6, 7)
                PT, rPT = PTs[par]
                for p in plist:
                    sl = p if sample else (t - 4 + p) % 5
                    bank = bA if p < 4 else bB
                    col = (p % 4) * 128
                    outp = ps[bank][:, col:col + 128]
                    extra = []
                    if p == 0:
                        extra.append((maskp0b[:], "maskp0b"))
                    elif p == 3:
                        extra.append((BT[:, hh, 0, :], "BT"))
                    elif p == 4:
                        extra.append((BT[:, hh, 1, :], "BT"))
                        if sample:
                            extra.append((maskSb[:], "maskSb"))
                    mm(outp, kring[par * 64:(par + 1) * 64, pr, sl * 128:(sl + 1) * 128], qT[par * 64:(par + 1) * 64, pr, :],
                       True, len(extra) == 0, r=["kring", "qT"], w=[PSN[bank]])
                    for ei, (eap, en) in enumerate(extra):
                        mm(outp, identb[:], eap, False, ei == len(extra) - 1, r=["identb", en], w=[PSN[bank]])
                yield
                p0 = plist[0]
                if p0 < 4:
                    act(PT[:, p0 * 128:512], ps[bA][:, p0 * 128:512], AF.Exp, r=[PSN[bA]], w=rPT)
                act(PT[:, 512:640], ps[bB][:, 0:128], AF.Exp, r=[PSN[bB]], w=rPT)
                yield
                ob = 5 if hh < 4 else 7
                j = hh % 4
                for p in plist:
                    sl = p if sample else (t - 4 + p) % 5
                    mm(ps[ob][:, 128 + j * 65:128 + (j + 1) * 65], PT[:, p * 128:(p + 1) * 128], vring[:, sl, hh, :],
                       p == plist[0], p == plist[-1], r=[rPT, "vring"], w=[PSN[ob]])
                yield
                if j == 3:
                    g4 = hh // 4
                    S.op('dve', lambda e, ob=ob, g4=g4: e.reciprocal(
                        rden[:, g4 * 4:(g4 + 1) * 4], ps[ob][:, 128:388].rearrange("p (h d) -> p h d", h=4)[:, :, 64]),
                        r=[PSN[ob]], w=rrden)
                    og4 = oag[:, g4 * 256:(g4 + 1) * 256]
                    tt('dve', og4.rearrange("p (h d) -> p h d", h=4), ps[ob][:, 128:388].rearrange("p (h d) -> p h d", h=4)[:, :, 0:64],
                       rden[:, g4 * 4:(g4 + 1) * 4].unsqueeze(2).to_broadcast([128, 4, 64]), ALU.mult, r=[PSN[ob], rrden], w=roag)
                    tt('dve', og4, og4, za_s[:, g4 * 256:(g4 + 1) * 256], ALU.mult, r=[roag, "za_s"], w=roag)
                    yield
            for c in range(4):
                tr(ps[4][:, c * 128:(c + 1) * 128], oag[:, c * 128:(c + 1) * 128], cf(0), r=[roag, "CF"], w=[PSN[4]])
            cp('act', oagT[:], v4(ps[4][:]), r=[PSN[4]], w=["oagT"])
            yield

        def g_pre():
            proj_tm(C_AB, 16, 0)
            cp('dve', small[:, 0:2, :], ps[0][:, 0:16].rearrange("p (a b) -> p a b", a=2), r=[PSN[0]], w=[SMALL])
            yield
            tt('dve', sm(2), sm(0), dtbs[:], ALU.add, r=[SMALL, "dtbs"], w=[SMALL])
            act(sm(2), sm(2), AF.Exp, r=[SMALL], w=[SMALL])
            act(sm(2), sm(2), AF.Ln, r=[SMALL], w=[SMALL], bias=1.0)
            tt('dve', sm(2), sm(2), negA[:], ALU.mult, r=[SMALL, "negA"], w=[SMALL])
            if sample:
                ts('dve', sm(2), sm(2), validcol, None, ALU.mult, None, r=[SMALL, "CF"], w=[SMALL])
            act(sm(3), sm(1), AF.Sigmoid, r=[SMALL], w=[SMALL])
            if sample:
                ts('dve', sm(3), sm(3), validcol, None, ALU.mult, None, r=[SMALL, "CF"], w=[SMALL])
            yield
            mm(ps[0][:, 16:24], cf(2), sm(2), True, True, r=["CF", SMALL], w=[PSN[0]])
            for c in range(2):
                mm(ps[c][:, 32:40], CF[c * 64:(c + 1) * 64, 1, :], small[c * 64:(c + 1) * 64, 2, :], True, True,
                   r=["CF", SMALL], w=[PSN[c]])
            cp('dve', sm(4), ps[0][:, 16:24], r=[PSN[0]], w=[SMALL])
            cp('dve', sm(5), ps[0][:, 32:40], r=[PSN[0]], w=[SMALL])
            cp('dve', sm(6), ps[1][:, 32:40], r=[PSN[1]], w=[SMALL])
            yield
            act(sm(7), sm(4), AF.Exp, r=[SMALL], w=[SMALL])
            act(small[:, 8:10, :], small[:, 5:7, :], AF.Exp, r=[SMALL], w=[SMALL])
            for c in range(2):
                rr = slice(c * 64, (c + 1) * 64)
                tt('dve', small[rr, 10, :], small[rr, 5 + c, :], small[rr, 4, :], ALU.subtract, r=[SMALL], w=[SMALL])
            act(sm(10), sm(10), AF.Exp, r=[SMALL], w=[SMALL])
            stt('dve', sm(11), sm(3), -1.0, sm(7), ALU.mult, ALU.mult, r=[SMALL], w=[SMALL])
            yield
            S.op('pool', lambda e: e.tensor_copy(xb[:, :, 0:3], xb[:, :, 128:131]), r=["xb"], w=["xb"])
            for g3 in range(3):
                for c in range(4):
                    for kc in range(8):
                        mm(ps[1 + g3][:, c * 128:(c + 1) * 128],
                           Win[:, kc, C_QKVB + g3 * 512 + c * 128:C_QKVB + g3 * 512 + (c + 1) * 128], hT[:, kc, :],
                           kc == 0, kc == 7, r=["Win", hTn], w=[PSN[1 + g3]])
                    yield
                cp('act' if g3 % 2 else 'dve', xb[:, g3 * 4:(g3 + 1) * 4, 3:131], v4(ps[1 + g3][:]), r=[PSN[1 + g3]], w=["xb"])
                yield
            proj_tm(C_ZB, 512, 0)
            act(zb_s[:], ps[0][:], AF.Silu, r=[PSN[0]], w=["zb_s"])
            yield
            if sample or t == NT - 1:
                lo = 32 if sample else 128
                dst = ncs[l, t] if sample else ncp[l]
                nct, rnct = cs.rearrange("p a b -> p (a b)"), rcs
                for c in range(12):
                    tr(ps[1 + c // 4][0:3, (c % 4) * 128:(c % 4 + 1) * 128], xb[:, c, lo:lo + 3], cf(0), r=["xb", "CF"], w=[PSN[1 + c // 4]])
                for g3 in range(3):
                    cp('dve', nct[0:3, g3 * 512:(g3 + 1) * 512], ps[1 + g3][0:3, :], r=[PSN[1 + g3]], w=rnct)
                dma(dst, nct[0:3, :], r=rnct, w=["ncout"], eng='sp')
                yield
            cgv = slice(8, 12)
            for i in range(4):
                wbc = cw[:, cgv, i:i + 1].to_broadcast([128, 4, 128])
                if i == 0:
                    tt('pool', cs[:, cgv, :], xb[:, cgv, 0:128], wbc, ALU.mult, r=["xb", "cw"], w=rcs_v)
                else:
                    tt('pool', ctv, xb[:, cgv, i:i + 128], wbc, ALU.mult, r=["xb", "cw"], w=rctv)
                    tt('pool', cs[:, cgv, :], cs[:, cgv, :], ctv, ALU.add, r=[rcs_v, rctv], w=rcs_v)
            cgq = slice(0, 8)
            for i in range(4):
                wbc = cw[:, cgq, i:i + 1].to_broadcast([128, 8, 128])
                if i == 0:
                    tt('dve', cs[:, cgq, :], xb[:, cgq, 0:128], wbc, ALU.mult, r=["xb", "cw"], w=rcs_qk)
                else:
                    tt('dve', ctmp, xb[:, cgq, i:i + 128], wbc, ALU.mult, r=["xb", "cw"], w=rct)
                    tt('dve', cs[:, cgq, :], cs[:, cgq, :], ctmp, ALU.add, r=[rcs_qk, rct], w=rcs_qk)
                yield
            act(cs, cs, AF.Silu, r=rcs, w=rcs)
            act(sq, cs[:, 0:8, :], AF.Square, r=rcs, w=rsq)
            yield
            for c in range(8):
                mm(ps[c // 4][:, (c % 4) * 128:(c % 4 + 1) * 128], bdonesb[:], sq[:, c, :], True, True,
                   r=["bdonesb", rsq], w=[PSN[c // 4]])
            yield
            for g2 in range(2):
                act(rinv[:, g2 * 4:(g2 + 1) * 4, :], v4(ps[g2][:]), AF.Sqrt, r=[PSN[g2]], w=rri, bias=NORM_EPS, scale=1.0)
            S.op('dve', lambda e: e.reciprocal(rinv, rinv), r=rri, w=rri)
            yield
            stt('dve', qh, cs[:, 0:4, :], 0.125, rinv[:, 0:4, :], ALU.mult, ALU.mult, r=[rcs, rri], w=rqh)
            tt('dve', khf, cs[:, 4:8, :], rinv[:, 4:8, :], ALU.mult, r=[rcs, rri], w=rkhf)
            cp('act', kh, khf, r=rkhf, w=rkh)
            yield
            for c in range(4):
                tr(ps[2][:, c * 128:(c + 1) * 128], khf[:, c, :], cf(0), r=[rkhf, "CF"], w=[PSN[2]])
                tr(ps[3][:, c * 128:(c + 1) * 128], cs[:, 8 + c, :], cf(0), r=[rcs, "CF"], w=[PSN[3]])
            yield
            tt('dve', v8(kdec), v8(ps[2][:]), sm(10).unsqueeze(2).to_broadcast([128, 8, 64]), ALU.mult, r=[PSN[2], SMALL], w=rkd)
            tt('dve', v8(vbeta), v8(ps[3][:]), sm(3).unsqueeze(2).to_broadcast([128, 8, 64]), ALU.mult, r=[PSN[3], SMALL], w=rvb)
            yield

        merge([g_attn(), g_pre()], list(cfg.get('W1', (1, 1))))
        AR.release(m_long)

        Ttb8, rTb8 = AR.get(512, BF16, (8, 128))
        QKT8, rQ8 = AR.get(512, BF16, (8, 128))
        Ttb_g = [(Ttb8[:, p_ * 4:(p_ + 1) * 4, :], rTb8) for p_ in range(2)]
        QKT_g = [(QKT8[:, p_ * 4:(p_ + 1) * 4, :], rQ8) for p_ in range(2)]
        m2 = AR.mark()
        Bm8, rB8 = AR.get(1024, F32, (8, 128))
        dec8, rdec8 = AR.get(1024, F32, (8, 128))
        Nf8, rNf8 = AR.get(1024, F32, (8, 128))
        N8, rN8 = AR.get(512, BF16, (8, 128))
        Nt8, rNt8 = AR.get(512, BF16, (8, 128))
        HO = [0, 2, 4, 6, 1, 3, 5, 7]
        B0, B1, B2 = PS[:, 1024:2048], PS[:, 2048:3072], PS[:, 3072:4096]
        rB0, rB1, rB2 = ["ps2", "ps3"], ["ps4", "ps5"], ["ps6", "ps7"]
        v88 = lambda ap: ap.rearrange("p (a b) -> p a b", a=8)
        pm = lambda row: small[:, row, :].rearrange("p (a b) -> p b a", b=2)
        x4 = lambda ap: ap.rearrange("p (b a) c -> p b a c", b=2)
        bnk = lambda a8: 2 + (a8 // 4)

        def g_w():
            tt('dve', x4(Bm8), cf(2).unsqueeze(1).unsqueeze(1).to_broadcast([128, 2, 4, 128]),
               pm(2).unsqueeze(3).to_broadcast([128, 2, 4, 128]), ALU.mult, r=["CF", SMALL], w=rB8)
            yield
            for a8, hh in enumerate(HO):
                outp = B0[:, a8 * 128:(a8 + 1) * 128]
                mm(outp, cf(1), Bm8[:, a8, :], True, False, r=["CF", rB8], w=[PSN[bnk(a8)]])
                mm(outp, identb[:], MBb[:], False, True, r=["identb", "MBb"], w=[PSN[bnk(a8)]])
                if a8 % 4 == 3:
                    yield
            for a8, hh in enumerate(HO):
                act(dec8[:, a8, :], B0[:, a8 * 128:(a8 + 1) * 128], AF.Exp, r=[PSN[bnk(a8)], SMALL], w=rdec8,
                    bias=small[:, 4, hh:hh + 1], scale=-1.0)
                if a8 % 4 == 3:
                    yield
            for a8, hh in enumerate(HO):
                pr, par = hh // 2, hh % 2
                mm(B1[:, a8 * 128:(a8 + 1) * 128], kh[par * 64:(par + 1) * 64, pr, :], kh[par * 64:(par + 1) * 64, pr, :],
                   True, True, r=[rkh], w=[PSN[2 + bnk(a8)]])
            yield
            tt('dve', Bm8, v88(B1), dec8, ALU.mult, r=[rB1, rdec8], w=rB8)
            tt('dve', x4(Bm8), x4(Bm8), pm(3).unsqueeze(3).to_broadcast([128, 2, 4, 128]), ALU.mult, r=[rB8, SMALL], w=rB8)
            tt('dve', Nf8, Bm8, cf(4).unsqueeze(1).to_broadcast([128, 8, 128]), ALU.mult, r=[rB8, "CF"], w=rNf8)
            yield
            cp('act', N8, Nf8, r=rNf8, w=rN8)
            for a8 in range(8):
                tr(B0[:, a8 * 128:(a8 + 1) * 128], Nf8[:, a8, :], cf(0), r=[rNf8, "CF"], w=[PSN[bnk(a8)]])
                if a8 % 4 == 3:
                    yield
            cp('act', Nt8, v88(B0), r=rB0, w=rNt8)
            tt('dve', Nf8, v88(B0), cf(0).unsqueeze(1).to_broadcast([128, 8, 128]), ALU.add, r=[rB0, "CF"], w=rNf8)
            yield
            cp('act', Ttb8, Nf8, r=rNf8, w=rTb8)
            yield
            for lev in range(5):
                for a8 in range(8):
                    mm(B0[:, a8 * 128:(a8 + 1) * 128], Nt8[:, a8, :], N8[:, a8, :], True, True, r=[rNt8, rN8], w=[PSN[bnk(a8)]])
                    if a8 % 4 == 3:
                        yield
                if lev < 4:
                    for a8 in range(8):
                        mm(B1[:, a8 * 128:(a8 + 1) * 128], N8[:, a8, :], Nt8[:, a8, :], True, True, r=[rNt8, rN8], w=[PSN[2 + bnk(a8)]])
                        if a8 % 4 == 3:
                            yield
                cp('act', N8, v88(B0), r=rB0, w=rN8)
                if lev < 4:
                    cp('dve', Nt8, v88(B1), r=rB1, w=rNt8)
                yield
                for a8 in range(8):
                    mm(B2[:, a8 * 128:(a8 + 1) * 128], N8[:, a8, :], Ttb8[:, a8, :], True, True, r=[rN8, rTb8], w=[PSN[4 + bnk(a8)]])
                    if a8 % 4 == 3:
                        yield
                tt('dve', Nf8, Nf8, v88(B2), ALU.add, r=[rNf8, rB2], w=rNf8)
                cp('act', Ttb8, Nf8, r=rNf8, w=rTb8)
                yield
            for a8, hh in enumerate(HO):
                pr, par = hh // 2, hh % 2
                mm(B0[:, a8 * 128:(a8 + 1) * 128], qh[par * 64:(par + 1) * 64, pr, :], kh[par * 64:(par + 1) * 64, pr, :],
                   True, True, r=[rqh, rkh], w=[PSN[bnk(a8)]])
            yield
            tt('dve', Bm8, v88(B0), dec8, ALU.mult, r=[rB0, rdec8], w=rB8)
            yield
            for a8 in range(8):
                tr(B1[:, a8 * 128:(a8 + 1) * 128], Bm8[:, a8, :], cf(0), r=[rB8, "CF"], w=[PSN[2 + bnk(a8)]])
                if a8 % 4 == 3:
                    yield
            cp('act', QKT8, v88(B1), r=rB1, w=rQ8)
            yield

        if next_ac is not None:
            merge([g_w(), next_ac], [1, 1])
        else:
            for _ in g_w():
                pass
        AR.release(m2)

        Zb, rZ = AR.get(256, BF16)
        vn, rvn = AR.get(256, BF16)
        qSe, rqs = AR.get(512)
        og, rog = AR.get(512)
        ztmp, rzt = og, rog
        mix, rmix = AR.top(1024)
        sgb, rsgb = AR.top(1024)
        sg, rsg = AR.top(512)

        def g_rec():
            for c in range(2):
                rr = slice(c * 64, (c + 1) * 64)
                bk = 4 + c * 2
                for pr in range(4):
                    mm(ps[bk][rr, pr * 128:(pr + 1) * 128], kh[:, pr, c * 64:(c + 1) * 64], Sb[:, pr, :], True, True,
                       r=[rkh, "Sb"], w=[PSN[bk]])
                for pr in range(4):
                    mm(ps[bk + 1][rr, pr * 128:(pr + 1) * 128], qh[:, pr, c * 64:(c + 1) * 64], Sb[:, pr, :], True, True,
                       r=[rqh, "Sb"], w=[PSN[bk + 1]])
                yield
                tt('dve', v8(ztmp[rr, :]), v8(ps[bk][rr, :]), small[rr, 11, :].unsqueeze(2).to_broadcast([64, 8, 64]), ALU.mult,
                   r=[PSN[bk], SMALL], w=rzt)
                tt('dve', Zb[rr, :], ztmp[rr, :], vbeta[rr, :], ALU.add, r=[rzt, rvb], w=rZ)
                tt('dve', v8(qSe[rr, :]), v8(ps[bk + 1][rr, :]), small[rr, 7, :].unsqueeze(2).to_broadcast([64, 8, 64]), ALU.mult,
                   r=[PSN[bk + 1], SMALL], w=rqs)
                yield
                for hh in range(8):
                    par, a = hh % 2, hh // 2
                    Ttb, rTb = Ttb_g[par]
                    mm(ps[bk][rr, hh * 64:(hh + 1) * 64], Ttb[rr, a, c * 64:(c + 1) * 64], Zb[rr, hh * 64:(hh + 1) * 64], True, True,
                       r=[rTb, rZ], w=[PSN[bk]])
                yield
                cp('act', vn[rr, :], ps[bk][rr, :], r=[PSN[bk]], w=rvn)
                yield
                for hh in range(8):
                    par, a = hh % 2, hh // 2
                    QKT, rQ = QKT_g[par]
                    mm(ps[bk + 1][rr, hh * 64:(hh + 1) * 64], QKT[rr, a, c * 64:(c + 1) * 64], vn[rr, hh * 64:(hh + 1) * 64], True, True,
                       r=[rQ, rvn], w=[PSN[bk + 1]])
                for pr in range(4):
                    mm(ps[bk][:, pr * 128:(pr + 1) * 128], kdec[rr, pr * 128:(pr + 1) * 128], vn[rr, pr * 128:(pr + 1) * 128], True, True,
                       r=[rkd, rvn], w=[PSN[bk]])
                yield
                S.op('dve', lambda e, c=c: e.tensor_tensor(
                    Sbd[:].rearrange("p a (b d) -> p a b d", b=2), Sbd[:].rearrange("p a (b d) -> p a b d", b=2),
                    small[:, 8 + c, :].rearrange("p (a b) -> p a b", b=2).unsqueeze(3).to_broadcast([128, 4, 2, 64]), ALU.mult),
                    r=["Sbd", SMALL], w=["Sbd"])
                for par in range(2):
                    pp = slice(par * 64, (par + 1) * 64)
                    tt('dve', Sbd[pp, :, par * 64:(par + 1) * 64], Sbd[pp, :, par * 64:(par + 1) * 64],
                       v4(ps[bk][pp, :])[:, :, par * 64:(par + 1) * 64], ALU.add, r=["Sbd", PSN[bk]], w=["Sbd"])
                cp('act', Sb[:], Sbd[:], r=["Sbd"], w=["Sb"])
                yield
                tt('dve', og[rr, :], ps[bk + 1][rr, :], qSe[rr, :], ALU.add, r=[PSN[bk + 1], rqs], w=rog)
                yield
            if sample or t == NT - 1:
                dst = ngs[l, t] if sample else ngp[l]
                for pr in range(4):
                    for par in range(2):
                        dma(dst[pr * 2 + par], Sbd[par * 64:(par + 1) * 64, pr, par * 64:(par + 1) * 64], r=["Sbd"], w=["ngout"], eng='sp')
            osq, ros = qSe, rqs
            act(osq, og, AF.Square, r=rog, w=ros)
            S.op('dve', lambda e: e.tensor_reduce(small[:, 12, :], v8(osq), AX.X, ALU.add), r=ros, w=[SMALL])
            act(sm(12), sm(12), AF.Sqrt, r=[SMALL], w=[SMALL], bias=NORM_EPS, scale=1.0 / 64.0)
            S.op('dve', lambda e: e.reciprocal(small[:, 12, :], small[:, 12, :]), r=[SMALL], w=[SMALL])
            yield
            tt('dve', v8(osq), v8(og), sm(12).unsqueeze(2).to_broadcast([128, 8, 64]), ALU.mult, r=[rog, SMALL], w=ros)
            tt('dve', v8(osq), v8(osq), gnws[:].unsqueeze(1).to_broadcast([128, 8, 64]), ALU.mult, r=[ros, "gnws"], w=ros)
            tt('dve', osq, osq, zb_s[:], ALU.mult, r=[ros, "zb_s"], w=ros)
            yield
            for c in range(4):
                tr(ps[4][:, c * 128:(c + 1) * 128], osq[:, c * 128:(c + 1) * 128], cf(0), r=[ros, "CF"], w=[PSN[4]])
            cp('act', obgT[:], v4(ps[4][:]), r=[PSN[4]], w=["obgT"])
            yield

        def g_early():
            for nb in range(2):
                for kc in range(4):
                    mm(ps[0][:], oagT[:, kc, :], Wa[:, kc, nb * 512:(nb + 1) * 512], kc == 0, kc == 3, r=["oagT", "Wa"], w=[PSN[0]])
                yield
                for kc in range(8):
                    mm(ps[2][:], hT[:, kc, :], Win[:, kc, C_GA + nb * 512:C_GA + (nb + 1) * 512], kc == 0, kc == 7,
                       r=["Win", hTn], w=[PSN[2]])
                    if kc == 3:
                        yield
                yield
                for kc in range(8):
                    mm(ps[3][:], hT[:, kc, :], Win[:, kc, C_GB + nb * 512:C_GB + (nb + 1) * 512], kc == 0, kc == 7,
                       r=["Win", hTn], w=[PSN[3]])
                    if kc == 3:
                        yield
                yield
                act(sg, ps[2][:], AF.Sigmoid, r=[PSN[2]], w=rsg)
                tt('dve', mix[:, nb * 512:(nb + 1) * 512], ps[0][:], sg, ALU.mult, r=[PSN[0], rsg], w=rmix)
                act(sgb[:, nb * 512:(nb + 1) * 512], ps[3][:], AF.Sigmoid, r=[PSN[3]], w=rsgb)
                yield

        merge([g_rec(), g_early()], list(cfg.get('W2', (1, 1))))
        AR.reset()

        mtmp, rmt = AR.get(512)
        for nb in range(2):
            for kc in range(4):
                mm(ps[1][:], obgT[:, kc, :], Wb[:, kc, nb * 512:(nb + 1) * 512], kc == 0, kc == 3, r=["obgT", "Wb"], w=[PSN[1]])
            tt('dve', mtmp, ps[1][:], sgb[:, nb * 512:(nb + 1) * 512], ALU.mult, r=[PSN[1], rsgb], w=rmt)
            tt('dve', mix[:, nb * 512:(nb + 1) * 512], mix[:, nb * 512:(nb + 1) * 512], mtmp, ALU.add, r=[rmix, rmt], w=rmix)
        mixT, rmT = AR.get(512, BF16, (8, 128))
        for half in range(2):
            for c in range(4):
                kc = half * 4 + c
                tr(ps[4 + half][:, c * 128:(c + 1) * 128], mix[:, kc * 128:(kc + 1) * 128], cf(0), r=[rmix, "CF"], w=[PSN[4 + half]])
            cp('act' if half else 'dve', mixT[:, half * 4:half * 4 + 4, :], v4(ps[4 + half][:]), r=[PSN[4 + half]], w=rmT)
        z, rz = AR.get(1024)
        for nb in range(2):
            for kc in range(8):
                mm(ps[6 + nb][:], mixT[:, kc, :], Wo[:, kc, nb * 512:(nb + 1) * 512], kc == 0, kc == 7, r=[rmT, "Wo"], w=[PSN[6 + nb]])
            stt('dve', z[:, nb * 512:(nb + 1) * 512], h[:, nb * 512:(nb + 1) * 512], ALPHA, ps[6 + nb][:], ALU.mult, ALU.add,
                r=[hn, PSN[6 + nb]], w=rz)
        layer_norm_rows(z, z, rz, rz)
        dma(hout[t * 128:(t + 1) * 128, :], z, r=rz, w=[hout_name], eng='sp')

    def merge(gens, weights):
        live = list(gens)
        while live:
            for g_, w_ in list(zip(live, weights)):
                for _ in range(w_):
                    try:
                        next(g_)
                    except StopIteration:
                        idx = live.index(g_)
                        live.pop(idx)
                        weights = weights[:idx] + weights[idx + 1:]
                        break

    tix = 0
    STOP = cfg.get('STOP')
    for l in range(DEPTH):
        if STOP == 'prologue':
            break
        if not cfg.get('NOLOAD'):
            load_layer(l)
        if STOP == 'load':
            break
        seq = [('p', t) for t in range(NT)] + [('s', t) for t in range(NSB)]
        issue_load(l, seq[0][0], seq[0][1], tix)
        ac_done = False
        for i, (kind, t) in enumerate(seq):
            pf = None
            nac = None
            if i + 1 < len(seq):
                k2, t2 = seq[i + 1]
                pf = (lambda k2=k2, t2=t2, tx=tix + 1, l=l: issue_load(l, k2, t2, tx))
                if k2 == 'p' and 1 <= t2 < NT - NKT and not cfg.get('NOOVL'):
                    nac = ac_stage(l, k2, t2, tix + 1)
            tile(l, kind, t, tix, ac_done=ac_done, prefetch=pf, next_ac=nac)
            ac_done = nac is not None
            tix += 1

    S.emit(nc, es)
    es.close()
    return nc


_IDX = bias_index()


def make_in_maps(cfg, inputs):
    T, NSB, DEPTH, NCORES = cfg['T'], cfg['NSB'], cfg['DEPTH'], cfg['NCORES']
    f = lambda a: np.ascontiguousarray(np.asarray(a, dtype=np.float32))
    x_prompt = f(inputs['x_prompt'])
    x_sample = f(inputs['x_sample'])
    NB = x_prompt.shape[0]
    bc = lambda v, n=128: np.ascontiguousarray(np.broadcast_to(np.asarray(v, np.float32)[None, :], (n, v.shape[-1])))
    rel = f(inputs['rel_bias'])
    btf = np.stack([np.stack([np.stack([rel[l, h][_IDX[pi]] for pi in range(3)]) for h in range(8)]) for l in range(DEPTH)])
    crow = np.stack([np.broadcast_to(rel[l, :, 191][None, :], (128, 8)) for l in range(DEPTH)])
    shared = dict(
        w_in=f(inputs['w_in']), w_a=f(inputs['w_branch_a']), w_b=f(inputs['w_branch_b']), w_o=f(inputs['w_out']),
        convw=np.ascontiguousarray(f(inputs['conv_w']).transpose(0, 2, 1)),
        ln0=np.stack([bc(f(inputs['ln0_g'])), bc(f(inputs['ln0_b']))]),
        lnl=np.stack([np.stack([bc(f(inputs['ln_g'])[l]), bc(f(inputs['ln_b'])[l])]) for l in range(DEPTH)]),
        alog=np.stack([bc(f(inputs['a_log'])[l]) for l in range(DEPTH)]),
        dtb=np.stack([bc(f(inputs['dt_bias'])[l]) for l in range(DEPTH)]),
        gnw=np.stack([bc(f(inputs['gdn_norm_w'])[l]) for l in range(DEPTH)]),
        bt=f(btf), crow=f(crow), consts=host_consts(),
    )
    in_maps = []
    for c in range(NCORES):
        b = c % NB
        sbs = list(range(c * NSB, (c + 1) * NSB))
        xs = np.zeros((NSB, 128, D), np.float32)
        xs[:, 0:32, :] = x_sample[sbs]
        m = dict(shared)
        m['xp'] = x_prompt[b]
        m['xs'] = xs.reshape(NSB * 128, D)
        m['ck'] = f(inputs['cache_attn_k'][:, sbs]).reshape(DEPTH, NSB, 512, 512)
        m['cv'] = f(inputs['cache_attn_v'][:, sbs]).reshape(DEPTH, NSB, 512, 512)
        m['sconv'] = f(inputs['state_conv'][:, sbs])
        m['sgdn'] = f(inputs['state_gdn'][:, sbs])
        in_maps.append(m)
    return in_maps


def gather(cfg, res, NB):
    T, NSB, DEPTH, NCORES = cfg['T'], cfg['NSB'], cfg['DEPTH'], cfg['NCORES']
    KEEP = min(512, T)
    R = res
    yp = np.stack([R[b]['yp'] for b in range(NB)])
    ys = np.concatenate([R[c]['ys'].reshape(NSB, 128, D)[:, 0:32] for c in range(NCORES)], axis=0)
    nkp = np.stack([R[b]['nkp'] for b in range(NB)], axis=1).reshape(DEPTH, NB, KEEP, 8, 64)
    nvp = np.stack([R[b]['nvp'] for b in range(NB)], axis=1).reshape(DEPTH, NB, KEEP, 8, 64)
    ncp = np.stack([R[b]['ncp'] for b in range(NB)], axis=1)
    ngp = np.stack([R[b]['ngp'] for b in range(NB)], axis=1)
    nks = np.concatenate([R[c]['nks'] for c in range(NCORES)], axis=1).reshape(DEPTH, NCORES * NSB, 32, 8, 64)
    nvs = np.concatenate([R[c]['nvs'] for c in range(NCORES)], axis=1).reshape(DEPTH, NCORES * NSB, 32, 8, 64)
    ncs = np.concatenate([R[c]['ncs'] for c in range(NCORES)], axis=1)
    ngs = np.concatenate([R[c]['ngs'] for c in range(NCORES)], axis=1)
    return tuple(np.ascontiguousarray(a, dtype=np.float32) for a in (yp, ys, nkp, nvp, ncp, ngp, nks, nvs, ncs, ngs))


_NC_CACHE = {}


def kernel(**inputs):
    cfg = dict(CFG)
    key = tuple(sorted(cfg.items()))
    if key not in _NC_CACHE:
        _NC_CACHE[key] = build(cfg)
    nc = _NC_CACHE[key]
    in_maps = make_in_maps(cfg, inputs)
    res = run_bass_kernel_spmd(nc, in_maps, core_ids=list(range(cfg['NCORES'])))
    return gather(cfg, res.results, np.asarray(inputs['x_prompt']).shape[0])
```

```python
import contextlib
import numpy as np
import concourse.bass as bass
import concourse.mybir as mybir
from concourse.bass_utils import run_bass_kernel_spmd

F32 = mybir.dt.float32
BF16 = mybir.dt.bfloat16
AF = mybir.ActivationFunctionType
ALU = mybir.AluOpType
AX = mybir.AxisListType

D = 1024
IN_DIM = 6160
ALPHA = 4.0 ** 0.25
LN_EPS = 1e-5
NORM_EPS = 1e-6
BIG = 30000.0
C_QA, C_KA, C_VA, C_ZA, C_QKVB, C_ZB, C_AB, C_GA, C_GB = 0, 512, 1024, 1536, 2048, 3584, 4096, 4112, 5136

CFG = dict(T=8192, NSB=4, DEPTH=2, NCORES=8)


class Sched:
    ENGS = ['pe', 'act', 'dve', 'pool', 'sp']
    KDMA = 8
    EPOCH = 4000
    NEP = 12

    def __init__(self):
        self.ops = {e: [] for e in self.ENGS}
        self.res = {}

    def op(self, eng, fn, r=(), w=(), dma=False):
        idx = len(self.ops[eng])
        deps = set()
        rl = _flat(r)
        wl = _flat(w)
        for n in rl:
            st = self.res.setdefault(n, [None, {}, []])
            if st[0] is not None:
                deps.add(st[0])
            if n.startswith('ps'):
                for e_, i_ in st[1].items():
                    if e_ != eng:
                        deps.add((e_, i_))
        for n in wl:
            st = self.res.setdefault(n, [None, {}, []])
            if st[0] is not None:
                deps.add(st[0])
            for e_, i_ in st[1].items():
                deps.add((e_, i_))
            for rd in st[2]:
                deps.add(rd)
        for n in rl:
            if dma:
                self.res[n][2].append((eng, idx))
            else:
                self.res[n][1][eng] = idx
        for n in wl:
            self.res[n] = [(eng, idx), {}, []]
        deps.discard((eng, idx))
        self.ops[eng].append(dict(fn=fn, deps=deps, dma=dma, sig=False))

    def finalize(self):
        for e in self.ENGS:
            for o in self.ops[e]:
                for (d, i) in o['deps']:
                    if d == 'pe' and e == 'pe':
                        continue
                    self.ops[d][i]['sig'] = True
        for e in self.ENGS:
            cnt = 0
            nd = 0
            for o in self.ops[e]:
                if o['dma']:
                    o['dsem'] = nd % self.KDMA
                    o['dval'] = 16 * (nd // self.KDMA + 1)
                    nd += 1
                elif o['sig']:
                    o['ep'] = cnt // self.EPOCH
                    o['sval'] = cnt % self.EPOCH + 1
                    cnt += 1
            assert cnt <= self.EPOCH * self.NEP, (e, cnt)

    def emit(self, nc, stack):
        self.finalize()
        sems = {e: [stack.enter_context(nc.semaphore(f"s_{e}_{k}")) for k in range(self.NEP)] for e in self.ENGS}
        dsems = {e: [stack.enter_context(nc.semaphore(f"d_{e}_{k}")) for k in range(self.KDMA)] for e in self.ENGS}
        block = stack.enter_context(nc.Block())
        ops = self.ops
        KD = self.KDMA

        def body(ename):
            def run(eng):
                waited = {}

                def wait(key, sem, val):
                    if waited.get(key, 0) >= val:
                        return
                    waited[key] = val
                    eng.wait_ge(sem, val)
                dcount = {}
                for o in ops[ename]:
                    for (d, i) in sorted(o['deps']):
                        po = ops[d][i]
                        if po['dma']:
                            wait(('d', d, po['dsem']), dsems[d][po['dsem']], po['dval'])
                        else:
                            if d == 'pe' and ename == 'pe':
                                continue
                            wait(('s', d, po['ep']), sems[d][po['ep']], po['sval'])
                    if o['dma']:
                        if o['dval'] > 16:
                            wait(('d', ename, o['dsem']), dsems[ename][o['dsem']], o['dval'] - 16)
                        inst = o['fn'](eng)
                        inst.then_inc(dsems[ename][o['dsem']], 16)
                        dcount[o['dsem']] = o['dval']
                    else:
                        inst = o['fn'](eng)
                        if o['sig']:
                            inst.then_inc(sems[ename][o['ep']], 1)
                for k, v in sorted(dcount.items()):
                    wait(('d', ename, k), dsems[ename][k], v)
            return run

        block.tensor(body('pe'))
        block.scalar(body('act'))
        block.vector(body('dve'))
        block.gpsimd(body('pool'))
        block.sync(body('sp'))


def _flat(x):
    out = []
    for a in x:
        if isinstance(a, (list, tuple)):
            out.extend(_flat(a))
        else:
            out.append(a)
    return out


def host_consts():
    i = np.arange(128)[:, None]
    j = np.arange(128)[None, :]
    same = (i // 64) == (j // 64)
    c = np.zeros((12, 128, 128), np.float32)
    c[0] = np.eye(128)
    c[1] = 1.0
    c[2] = (same & (i <= j))
    c[3] = np.where(same & (j <= i), 0.0, BIG)
    c[4] = np.where(same & (j < i), -1.0, 0.0)
    c[5] = same
    c[6] = np.where(i >= 32, -BIG, 0.0) * np.ones((1, 128))
    c[7][:, 0] = (np.arange(128) < 32)
    c[7][:, 1] = 1.0
    c[8] = np.where((i < 64) & (j >= 64), -BIG, 0.0)
    c[9] = np.where((i >= 64) & (j < 64), -BIG, 0.0)
    c[10] = np.where(same & (j <= i), 1.0, 0.0)
    return np.ascontiguousarray(c.transpose(1, 0, 2).reshape(128, 12 * 128))


def bias_index():
    kk = np.arange(128)[:, None]
    qq = np.arange(128)[None, :]
    out = []
    for p in (0, 3, 4):
        out.append(np.clip(qq - kk + (4 - p) * 128, -63, 128) + 63)
    return out


def build(cfg):
    T, NSB, DEPTH = cfg['T'], cfg['NSB'], cfg['DEPTH']
    NT = T // 128
    KEEP = min(512, T)
    NKT = KEEP // 128
    nc = bass.Bass("TRN2", target_bir_lowering=False)
    S = Sched()
    es = contextlib.ExitStack()

    def din(name, shape):
        return nc.dram_tensor(name, list(shape), F32, kind="ExternalInput").ap()

    def dout(name, shape):
        return nc.dram_tensor(name, list(shape), F32, kind="ExternalOutput").ap()

    def dint(name, shape):
        return nc.dram_tensor(name, list(shape), F32, kind="Internal").ap()

    xp = din("xp", [T, D])
    xs = din("xs", [NSB * 128, D])
    ck = din("ck", [DEPTH, NSB, 512, 512])
    cv = din("cv", [DEPTH, NSB, 512, 512])
    sconv = din("sconv", [DEPTH, NSB, 3, 1536])
    sgdn = din("sgdn", [DEPTH, NSB, 8, 64, 64])
    w_in = din("w_in", [DEPTH, D, IN_DIM])
    w_a = din("w_a", [DEPTH, 512, D])
    w_b = din("w_b", [DEPTH, 512, D])
    w_o = din("w_o", [DEPTH, D, D])
    convw = din("convw", [DEPTH, 1536, 4])
    ln0 = din("ln0", [2, 128, D])
    lnl = din("lnl", [DEPTH, 2, 128, D])
    alog = din("alog", [DEPTH, 128, 8])
    dtb = din("dtb", [DEPTH, 128, 8])
    gnw = din("gnw", [DEPTH, 128, 64])
    bt = din("bt", [DEPTH, 8, 3, 128, 128])
    crow = din("crow", [DEPTH, 128, 8])
    consts = din("consts", [128, 12 * 128])

    yp = dout("yp", [T, D])
    ys = dout("ys", [NSB * 128, D])
    nkp = dout("nkp", [DEPTH, KEEP, 512])
    nvp = dout("nvp", [DEPTH, KEEP, 512])
    ncp = dout("ncp", [DEPTH, 3, 1536])
    ngp = dout("ngp", [DEPTH, 8, 64, 64])
    nks = dout("nks", [DEPTH, NSB, 32, 512])
    nvs = dout("nvs", [DEPTH, NSB, 32, 512])
    ncs = dout("ncs", [DEPTH, NSB, 3, 1536])
    ngs = dout("ngs", [DEPTH, NSB, 8, 64, 64])
    hbuf_p = [dint("h0p", [T, D]), dint("h1p", [T, D])]
    hbuf_s = [dint("h0s", [NSB * 128, D]), dint("h1s", [NSB * 128, D])]

    def sb(name, shape, dt=F32):
        return es.enter_context(nc.sbuf_tensor(name, list(shape), dt))

    Win = sb("Win", [128, 8, IN_DIM], BF16)
    Wa = sb("Wa", [128, 4, D], BF16)
    Wb = sb("Wb", [128, 4, D], BF16)
    Wo = sb("Wo", [128, 8, D], BF16)
    CF = sb("CF", [128, 6, 128])
    identb = sb("identb", [128, 128], BF16)
    bdonesb = sb("bdonesb", [128, 128], BF16)
    lnG = sb("lnG", [128, D])
    lnB = sb("lnB", [128, D])
    hx = [sb("hx0", [128, D]), sb("hx1", [128, D])]
    hTs = [sb("hT0", [128, 8, 128], BF16), sb("hT1", [128, 8, 128], BF16)]
    qT = sb("qT", [128, 4, 128], BF16)
    kring = sb("kring", [128, 4, 640], BF16)
    vring = sb("vring", [128, 5, 8, 65], BF16)
    BT = sb("BT", [128, 8, 2, 128], BF16)
    maskp0b = sb("maskp0b", [128, 128], BF16)
    maskSb = sb("maskSb", [128, 128], BF16)
    MBb = sb("MBb", [128, 128], BF16)
    crows = sb("crows", [128, 8])
    za_s = sb("za_s", [128, 512], BF16)
    zb_s = sb("zb_s", [128, 512], BF16)
    xb = sb("xb", [128, 12, 131])
    cw = sb("cw", [128, 12, 4])
    Sbd = sb("Sbd", [128, 4, 128])
    Sb = sb("Sb", [128, 4, 128], BF16)
    oagT = sb("oagT", [128, 4, 128], BF16)
    obgT = sb("obgT", [128, 4, 128], BF16)
    small = sb("small", [128, 24, 8])
    negA = sb("negA", [128, 8])
    dtbs = sb("dtbs", [128, 8])
    gnws = sb("gnws", [128, 64])
    ARW = 6400
    arena = sb("arena", [128, ARW])
    arena_b = arena.bitcast(BF16)

    class Arena:
        def __init__(self):
            self.off = 0

        def reset(self):
            self.off = 0
            self.topoff = ARW

        def mark(self):
            return self.off

        def top(self, words):
            self.topoff = getattr(self, 'topoff', ARW) - words
            o = self.topoff
            res = [f"ar{g}" for g in range(o // 256, (o + words - 1) // 256 + 1)]
            return arena[:, o:o + words], res

        def release(self, m):
            self.off = m

        def get(self, words, dt=F32, shape=None):
            o = self.off
            self.off += words
            assert self.off <= ARW, self.off
            res = [f"ar{g}" for g in range(o // 256, (o + words - 1) // 256 + 1)]
            if dt == F32:
                ap = arena[:, o:o + words]
            else:
                ap = arena_b[:, 2 * o:2 * o + 2 * words]
            if shape is not None and len(shape) == 2:
                ap = ap.rearrange("p (a b) -> p a b", a=shape[0])
            return ap, res
    AR = Arena()

    ps = [es.enter_context(nc.psum_tensor(f"ps{i}", [128, 512], F32)) for i in range(8)]
    PSN = [f"ps{i}" for i in range(8)]

    def cf(i):
        return CF[:, i, :]

    def dma(out, in_, r, w, eng='sp', slow=False):
        if slow:
            S.op(eng, lambda e: e.dma_start(out=out, in_=in_, allow_slow_non_contiguous=True), r=r, w=w, dma=True)
        else:
            S.op(eng, lambda e: e.dma_start(out=out, in_=in_), r=r, w=w, dma=True)

    def mm(out, lhsT, rhs, start, stop, r, w, tp=None):
        if tp is None:
            S.op('pe', lambda e: e.matmul(out, lhsT, rhs, start=start, stop=stop), r=r, w=w)
        else:
            S.op('pe', lambda e: e.matmul(out, lhsT, rhs, start=start, stop=stop, tile_position=tp), r=r, w=w)

    def tr(out, in_, ident, r, w):
        S.op('pe', lambda e: e.transpose(out, in_, ident), r=r, w=w)

    def act(out, in_, func, r, w, bias=None, scale=None, accum=None):
        kw = {}
        if bias is not None:
            kw['bias'] = bias
        if scale is not None:
            kw['scale'] = scale
        if accum is not None:
            kw['accum_out'] = accum
        S.op('act', lambda e: e.activation(out, in_, func, **kw), r=r, w=w)

    def ts(eng, out, in0, s1, s2, op0, op1, r, w):
        if op1 is None:
            S.op(eng, lambda e: e.tensor_scalar(out, in0, s1, None, op0), r=r, w=w)
        else:
            S.op(eng, lambda e: e.tensor_scalar(out, in0, s1, s2, op0, op1), r=r, w=w)

    def tt(eng, out, in0, in1, op, r, w):
        S.op(eng, lambda e: e.tensor_tensor(out, in0, in1, op), r=r, w=w)

    def stt(eng, out, in0, sc, in1, op0, op1, r, w):
        S.op(eng, lambda e: e.scalar_tensor_tensor(out, in0, sc, in1, op0, op1), r=r, w=w)

    def cp(eng, out, in_, r, w):
        if eng == 'act':
            S.op('act', lambda e: e.activation(out, in_, AF.Copy), r=r, w=w)
        else:
            S.op(eng, lambda e: e.tensor_copy(out, in_), r=r, w=w)

    c3 = consts.rearrange("p (a b) -> p a b", a=12)
    dma(CF[:, 0:5, :], c3[:, 0:5, :], r=[], w=["CF"])
    dma(CF[:, 5, :], c3[:, 7, :], r=[], w=["CF"])
    VALID_S = CF[:, 5, 0:1]
    VALID_P = CF[:, 5, 1:2]
    AR.reset()
    stg0, rs0 = AR.get(512, F32, (4, 128))
    dma(stg0[:, 0, :], c3[:, 5, :], r=[], w=rs0)
    dma(stg0[:, 1, :], c3[:, 6, :], r=[], w=rs0)
    dma(stg0[:, 2, :], c3[:, 8, :], r=[], w=rs0)
    dma(stg0[:, 3, :], c3[:, 9, :], r=[], w=rs0)
    maskp4f = sb("maskp4f", [128, 128])
    cp('dve', identb[:], cf(0), r=["CF"], w=["identb"])
    cp('dve', MBb[:], cf(3), r=["CF"], w=["MBb"])
    cp('dve', bdonesb[:], stg0[:, 0, :], r=rs0, w=["bdonesb"])
    cp('dve', maskSb[:], stg0[:, 1, :], r=rs0, w=["maskSb"])
    cp('dve', maskp0b[:], stg0[:, 2, :], r=rs0, w=["maskp0b"])
    cp('dve', maskp4f[:], stg0[:, 3, :], r=rs0, w=["maskp4f"])
    S.op('dve', lambda e: e.memset(vring[:], 1.0), r=[], w=["vring"])

    def layer_norm_rows(src, dst, rs, rd):
        st = small[:, 16:22, :].rearrange("p a b -> p (a b)")
        stats = st[:, 0:12]
        mv = st[:, 12:14]
        rstd = st[:, 14:15]
        xr = src.rearrange("p (c f) -> p c f", c=2)
        S.op('dve', lambda e: e.bn_stats(stats[:, 0:6], xr[:, 0, :]), r=[rs], w=["lnst"])
        S.op('dve', lambda e: e.bn_stats(stats[:, 6:12], xr[:, 1, :]), r=[rs], w=["lnst"])
        S.op('dve', lambda e: e.bn_aggr(mv, stats.rearrange("p (c s) -> p c s", c=2)), r=["lnst"], w=["lnst"])
        act(rstd, mv[:, 1:2], AF.Ln, r=["lnst"], w=["lnst"], bias=LN_EPS, scale=1.0)
        act(rstd, rstd, AF.Exp, r=["lnst"], w=["lnst"], scale=-0.5)
        stt('dve', dst, src, mv[:, 0:1], lnG[:], ALU.subtract, ALU.mult, r=[rs, "lnst", "lnG"], w=[rd])
        stt('dve', dst, dst, rstd, lnB[:], ALU.mult, ALU.add, r=[rd, "lnst", "lnB"], w=[rd])

    dma(lnG[:], ln0[0], r=[], w=["lnG"])
    dma(lnB[:], ln0[1], r=[], w=["lnB"])
    pbufs = [AR.get(1024) for _ in range(4)]
    tiles = [(xp, hbuf_p[0], t, "h0p") for t in range(NT)] + [(xs, hbuf_s[0], t, "h0s") for t in range(NSB)]
    for i, (srcd, dstd, t, rn) in enumerate(tiles):
        b, rb = pbufs[i % 4]
        dma(b, srcd[t * 128:(t + 1) * 128, :], r=[], w=rb)
        layer_norm_rows(b, b, rb, rb)
        dma(dstd[t * 128:(t + 1) * 128, :], b, r=rb, w=[f"{rn}_{t}"], eng='sp')

    cast_rr = [0]

    def cast(out, in_, r, w):
        k = cast_rr[0] % 3
        cast_rr[0] += 1
        cp(['pool', 'act', 'dve'][k], out, in_, r, w)

    def load_layer(l):
        AR.reset()
        stg = [AR.get(1540), AR.get(1540)]
        n = 0
        for kc in range(8):
            for cb in range(4):
                s_, r_ = stg[n % 2]
                n += 1
                dma(s_, w_in[l, kc * 128:(kc + 1) * 128, cb * 1540:(cb + 1) * 1540], r=[], w=r_)
                cast(Win[:, kc, cb * 1540:(cb + 1) * 1540], s_, r=r_, w=["Win"])
        for (wsrc, wdst, wn, nk) in ((w_a, Wa, "Wa", 4), (w_b, Wb, "Wb", 4), (w_o, Wo, "Wo", 8)):
            for kc in range(nk):
                s_, r_ = stg[n % 2]
                n += 1
                dma(s_[:, 0:1024], wsrc[l, kc * 128:(kc + 1) * 128, :], r=[], w=r_)
                cast(wdst[:, kc, :], s_[:, 0:1024], r=r_, w=[wn])
        dma(lnG[:], lnl[l, 0], r=[], w=["lnG"])
        dma(lnB[:], lnl[l, 1], r=[], w=["lnB"])
        dma(cw[:], convw[l].rearrange("(c p) i -> p c i", p=128), r=[], w=["cw"])
        dma(dtbs[:], dtb[l], r=[], w=["dtbs"])
        dma(gnws[:], gnw[l], r=[], w=["gnws"])
        dma(crows[:], crow[l], r=[], w=["crows"])
        dma(negA[:], alog[l], r=[], w=["negA"])
        act(negA[:], negA[:], AF.Exp, r=["negA"], w=["negA"])
        ts('dve', negA[:], negA[:], -1.0, None, ALU.mult, None, r=["negA"], w=["negA"])
        for h in range(8):
            for pi in (1, 2):
                s_, r_ = stg[n % 2]
                n += 1
                dma(s_[:, 0:128], bt[l, h, pi], r=[], w=r_)
                if pi == 1:
                    ts('dve', BT[:, h, 0, :], s_[:, 0:128], crows[:, h:h + 1], None, ALU.subtract, None, r=[r_, "crows"], w=["BT"])
                else:
                    stt('dve', BT[:, h, 1, :], s_[:, 0:128], crows[:, h:h + 1], maskp4f[:], ALU.subtract, ALU.add,
                        r=[r_, "crows", "maskp4f"], w=["BT"])

    def issue_load(l, kind, t, tix):
        hin_ = (hbuf_s if kind == 's' else hbuf_p)[l]
        dma(hx[tix % 2][:], hin_[t * 128:(t + 1) * 128, :], r=[f"h{l}{kind}_{t}"], w=[f"hx{tix % 2}"])

    def ac_stage(l, kind, t, tix):
        sample = (kind == 's')
        h = hx[tix % 2]
        hn = f"hx{tix % 2}"
        hT = hTs[tix % 2]
        hTn = f"hT{tix % 2}"
        first = sample or t == 0
        slot = 4 if sample else t % 5
        v4 = lambda ap: ap.rearrange("p (a b) -> p a b", a=4)
        v8 = lambda ap: ap.rearrange("p (h d) -> p h d", h=8)
        for half in range(2):
            for c in range(4):
                kc = half * 4 + c
                tr(ps[half][:, c * 128:(c + 1) * 128], h[:, kc * 128:(kc + 1) * 128], cf(0), r=[hn, "CF"], w=[PSN[half]])
            cp('act' if half else 'dve', hT[:, half * 4:half * 4 + 4, :], v4(ps[half][:]), r=[PSN[half]], w=[hTn])
            yield

        def proj_fm(col0, nch, bank):
            for c in range(nch):
                for kc in range(8):
                    mm(ps[bank][:, c * 128:(c + 1) * 128], Win[:, kc, col0 + c * 128:col0 + (c + 1) * 128], hT[:, kc, :],
                       kc == 0, kc == 7, r=["Win", hTn], w=[PSN[bank]])
                yield

        def proj_tm(col0, ncol, bank):
            for kc in range(8):
                mm(ps[bank][:, 0:ncol], hT[:, kc, :], Win[:, kc, col0:col0 + ncol], kc == 0, kc == 7,
                   r=["Win", hTn], w=[PSN[bank]])
                if kc == 3:
                    yield
            yield

        if sample:
            AR.reset()
            sbufs = [AR.get(512) for _ in range(4)]
            for blk in range(4):
                stg, rstg = sbufs[(2 * blk) % 4]
                dma(stg, ck[l, t, blk * 128:(blk + 1) * 128, :], r=[], w=rstg)
                for c in range(4):
                    tr(ps[2][:, c * 128:(c + 1) * 128], stg[:, c * 128:(c + 1) * 128], cf(0), r=[rstg, "CF"], w=[PSN[2]])
                cp('dve', kring[:, :, blk * 128:(blk + 1) * 128], v4(ps[2][:]), r=[PSN[2]], w=["kring"])
                stg2, rstg2 = sbufs[(2 * blk + 1) % 4]
                dma(stg2, cv[l, t, blk * 128:(blk + 1) * 128, :], r=[], w=rstg2)
                cp('act', vring[:, blk, :, 0:64], v8(stg2), r=rstg2, w=["vring"])
            AR.reset()
            sct, rsct = AR.get(1536)
            dma(sct[0:3, :], sconv[l, t], r=[], w=rsct)
            for c in range(12):
                tr(ps[3][:, c * 3:(c + 1) * 3], sct[0:3, c * 128:(c + 1) * 128], CF[0:3, 0, 0:3], r=[rsct, "CF"], w=[PSN[3]])
            cp('dve', xb[:, :, 128:131], ps[3][:, 0:36].rearrange("p (c i) -> p c i", c=12), r=[PSN[3]], w=["xb"])
            AR.reset()
            S.op('dve', lambda e: e.memset(Sbd[:], 0.0), r=[], w=["Sbd"])
            for pr in range(4):
                for par in range(2):
                    dma(Sbd[par * 64:(par + 1) * 64, pr, par * 64:(par + 1) * 64], sgdn[l, t, pr * 2 + par], r=[], w=["Sbd"])
            cp('act', Sb[:], Sbd[:], r=["Sbd"], w=["Sb"])
        elif first:
            S.op('dve', lambda e: e.memset(xb[:, :, 128:131], 0.0), r=[], w=["xb"])
            S.op('dve', lambda e: e.memset(Sbd[:], 0.0), r=[], w=["Sbd"])
            S.op('dve', lambda e: e.memset(Sb[:], 0.0), r=[], w=["Sb"])

        yield from proj_fm(C_QA, 4, 0)
        ts('dve', qT[:], v4(ps[0][:]), 0.125, None, ALU.mult, None, r=[PSN[0]], w=["qT"])
        yield from proj_fm(C_KA, 4, 1)
        cp('act', kring[:, :, slot * 128:(slot + 1) * 128], v4(ps[1][:]), r=[PSN[1]], w=["kring"])
        yield from proj_tm(C_VA, 512, 0)
        cp('dve', vring[:, slot, :, 0:64], v8(ps[0][:]), r=[PSN[0]], w=["vring"])
        want_kv = sample or (t >= NT - NKT)
        if want_kv:
            AR.reset()
            vst, rv = AR.get(512)
            cp('act', vst, ps[0][:], r=[PSN[0]], w=rv)
            if sample:
                dma(nvs[l, t], vst[0:32, :], r=rv, w=["nvs"], eng='sp')
            else:
                o0 = (t - (NT - NKT)) * 128
                dma(nvp[l, o0:o0 + 128, :], vst, r=rv, w=["nvp"], eng='sp')
        yield from proj_tm(C_ZA, 512, 1)
        act(za_s[:], ps[1][:], AF.Silu, r=[PSN[1]], w=["za_s"])
        yield
        if want_kv:
            yield from proj_tm(C_KA, 512, 0)
            kst, rk = AR.get(512)
            cp('act', kst, ps[0][:], r=[PSN[0]], w=rk)
            if sample:
                dma(nks[l, t], kst[0:32, :], r=rk, w=["nks"], eng='sp')
            else:
                o0 = (t - (NT - NKT)) * 128
                dma(nkp[l, o0:o0 + 128, :], kst, r=rk, w=["nkp"], eng='sp')
            yield

    def tile(l, kind, t, tix, ac_done=False, prefetch=None, next_ac=None):
        sample = (kind == 's')
        last_layer = (l == DEPTH - 1)
        if last_layer:
            hout = ys if sample else yp
            hout_name = f"y{kind}_{t}"
        else:
            hout = (hbuf_s if sample else hbuf_p)[l + 1]
            hout_name = f"h{l + 1}{kind}_{t}"
        h = hx[tix % 2]
        hn = f"hx{tix % 2}"
        hT = hTs[tix % 2]
        hTn = f"hT{tix % 2}"
        validcol = VALID_S if sample else VALID_P
        SMALL = "small"
        sm = lambda i: small[:, i, :]
        v4 = lambda ap: ap.rearrange("p (a b) -> p a b", a=4)
        v8 = lambda ap: ap.rearrange("p (h d) -> p h d", h=8)

        def proj_tm(col0, ncol, bank):
            for kc in range(8):
                mm(ps[bank][:, 0:ncol], hT[:, kc, :], Win[:, kc, col0:col0 + ncol], kc == 0, kc == 7,
                   r=["Win", hTn], w=[PSN[bank]])

        if not ac_done:
            for _ in ac_stage(l, kind, t, tix):
                pass
        if prefetch is not None:
            prefetch()
        AR.reset()

        AR.reset()
        qh, rqh = AR.get(256, BF16, (4, 128))
        kh, rkh = AR.get(256, BF16, (4, 128))
        kdec, rkd = AR.get(256, BF16)
        vbeta, rvb = AR.get(512)
        m_long = AR.mark()
        PTs = [AR.get(320, BF16), AR.get(320, BF16)]
        oag, roag = AR.get(512)
        rden, rrden = AR.get(8)
        cs, rcs = AR.get(1536, F32, (12, 128))
        rcs_qk, rcs_v = rcs[:4], rcs[4:]
        ctmp, rct = AR.get(1024, F32, (8, 128))
        AR.off -= 1024
        sq, rsq = AR.get(512, BF16, (8, 128))
        AR.off -= 512
        rinv, rri = AR.get(1024, F32, (8, 128))
        ctv, rctv = AR.get(512, F32, (4, 128))
        khf, rkhf = AR.get(512, F32, (4, 128))
        if sample:
            plist = [0, 1, 2, 3, 4]
        else:
            plist = [p for p in range(5) if t - 4 + p >= 0]

        def g_attn():
            for hh in range(8):
                pr, par = hh // 2, hh % 2
                bA, bB = (4, 5) if par == 0 else (6, 7)
                PT, rPT = PTs[par]
                for p in plist:
                    sl = p if sample else (t - 4 + p) % 5
                    bank = bA if p < 4 else bB
                    col = (p % 4) * 128
                    outp = ps[bank][:, col:col + 128]
                    extra = []
                    if p == 0:
                        extra.append((maskp0b[:], "maskp0b"))
                    elif p == 3:
                        extra.append((BT[:, hh, 0, :], "BT"))
                    elif p == 4:
                        extra.append((BT[:, hh, 1, :], "BT"))
                        if sample:
                            extra.append((maskSb[:], "maskSb"))
                    mm(outp, kring[par * 64:(par + 1) * 64, pr, sl * 128:(sl + 1) * 128], qT[par * 64:(par + 1) * 64, pr, :],
                       True, len(extra) == 0, r=["kring", "qT"], w=[PSN[bank]])
                    for ei, (eap, en) in enumerate(extra):
                        mm(outp, identb[:], eap, False, ei == len(extra) - 1, r=["identb", en], w=[PSN[bank]])
                yield
                p0 = plist[0]
                if p0 < 4:
                    act(PT[:, p0 * 128:512], ps[bA][:, p0 * 128:512], AF.Exp, r=[PSN[bA]], w=rPT)
                act(PT[:, 512:640], ps[bB][:, 0:128], AF.Exp, r=[PSN[bB]], w=rPT)
                yield
                ob = 5 if hh < 4 else 7
                j = hh % 4
                for p in plist:
                    sl = p if sample else (t - 4 + p) % 5
                    mm(ps[ob][:, 128 + j * 65:128 + (j + 1) * 65], PT[:, p * 128:(p + 1) * 128], vring[:, sl, hh, :],
                       p == plist[0], p == plist[-1], r=[rPT, "vring"], w=[PSN[ob]])
                yield
                if j == 3:
                    g4 = hh // 4
                    S.op('dve', lambda e, ob=ob, g4=g4: e.reciprocal(
                        rden[:, g4 * 4:(g4 + 1) * 4], ps[ob][:, 128:388].rearrange("p (h d) -> p h d", h=4)[:, :, 64]),
                        r=[PSN[ob]], w=rrden)
                    og4 = oag[:, g4 * 256:(g4 + 1) * 256]
                    tt('dve', og4.rearrange("p (h d) -> p h d", h=4), ps[ob][:, 128:388].rearrange("p (h d) -> p h d", h=4)[:, :, 0:64],
                       rden[:, g4 * 4:(g4 + 1) * 4].unsqueeze(2).to_broadcast([128, 4, 64]), ALU.mult, r=[PSN[ob], rrden], w=roag)
                    tt('dve', og4, og4, za_s[:, g4 * 256:(g4 + 1) * 256], ALU.mult, r=[roag, "za_s"], w=roag)
                    yield
            for c in range(4):
                tr(ps[4][:, c * 128:(c + 1) * 128], oag[:, c * 128:(c + 1) * 128], cf(0), r=[roag, "CF"], w=[PSN[4]])
            cp('act', oagT[:], v4(ps[4][:]), r=[PSN[4]], w=["oagT"])
            yield

        def g_pre():
            proj_tm(C_AB, 16, 0)
            cp('dve', small[:, 0:2, :], ps[0][:, 0:16].rearrange("p (a b) -> p a b", a=2), r=[PSN[0]], w=[SMALL])
            yield
            tt('dve', sm(2), sm(0), dtbs[:], ALU.add, r=[SMALL, "dtbs"], w=[SMALL])
            act(sm(2), sm(2), AF.Exp, r=[SMALL], w=[SMALL])
            act(sm(2), sm(2), AF.Ln, r=[SMALL], w=[SMALL], bias=1.0)
            tt('dve', sm(2), sm(2), negA[:], ALU.mult, r=[SMALL, "negA"], w=[SMALL])
            if sample:
                ts('dve', sm(2), sm(2), validcol, None, ALU.mult, None, r=[SMALL, "CF"], w=[SMALL])
            act(sm(3), sm(1), AF.Tanh, r=[SMALL], w=[SMALL], scale=0.5)
            ts('dve', sm(3), sm(3), 0.5, 0.5, ALU.mult, ALU.add, r=[SMALL], w=[SMALL])
            if sample:
                ts('dve', sm(3), sm(3), validcol, None, ALU.mult, None, r=[SMALL, "CF"], w=[SMALL])
            yield
            mm(ps[0][:, 16:24], cf(2), sm(2), True, True, r=["CF", SMALL], w=[PSN[0]])
            for c in range(2):
                mm(ps[c][:, 32:40], CF[c * 64:(c + 1) * 64, 1, :], small[c * 64:(c + 1) * 64, 2, :], True, True,
                   r=["CF", SMALL], w=[PSN[c]])
            cp('dve', sm(4), ps[0][:, 16:24], r=[PSN[0]], w=[SMALL])
            cp('dve', sm(5), ps[0][:, 32:40], r=[PSN[0]], w=[SMALL])
            cp('dve', sm(6), ps[1][:, 32:40], r=[PSN[1]], w=[SMALL])
            yield
            act(sm(7), sm(4), AF.Exp, r=[SMALL], w=[SMALL])
            act(small[:, 8:10, :], small[:, 5:7, :], AF.Exp, r=[SMALL], w=[SMALL])
            for c in range(2):
                rr = slice(c * 64, (c + 1) * 64)
                tt('dve', small[rr, 10, :], small[rr, 5 + c, :], small[rr, 4, :], ALU.subtract, r=[SMALL], w=[SMALL])
            act(sm(10), sm(10), AF.Exp, r=[SMALL], w=[SMALL])
            stt('dve', sm(11), sm(3), -1.0, sm(7), ALU.mult, ALU.mult, r=[SMALL], w=[SMALL])
            yield
            S.op('pool', lambda e: e.tensor_copy(xb[:, :, 0:3], xb[:, :, 128:131]), r=["xb"], w=["xb"])
            for g3 in range(3):
                for c in range(4):
                    for kc in range(8):
                        mm(ps[1 + g3][:, c * 128:(c + 1) * 128],
                           Win[:, kc, C_QKVB + g3 * 512 + c * 128:C_QKVB + g3 * 512 + (c + 1) * 128], hT[:, kc, :],
                           kc == 0, kc == 7, r=["Win", hTn], w=[PSN[1 + g3]])
                    yield
                cp('act' if g3 % 2 else 'dve', xb[:, g3 * 4:(g3 + 1) * 4, 3:131], v4(ps[1 + g3][:]), r=[PSN[1 + g3]], w=["xb"])
                yield
            proj_tm(C_ZB, 512, 0)
            act(zb_s[:], ps[0][:], AF.Silu, r=[PSN[0]], w=["zb_s"])
            yield
            if sample or t == NT - 1:
                lo = 32 if sample else 128
                dst = ncs[l, t] if sample else ncp[l]
                nct, rnct = cs.rearrange("p a b -> p (a b)"), rcs
                for c in range(12):
                    tr(ps[1 + c // 4][0:3, (c % 4) * 128:(c % 4 + 1) * 128], xb[:, c, lo:lo + 3], cf(0), r=["xb", "CF"], w=[PSN[1 + c // 4]])
                for g3 in range(3):
                    cp('dve', nct[0:3, g3 * 512:(g3 + 1) * 512], ps[1 + g3][0:3, :], r=[PSN[1 + g3]], w=rnct)
                dma(dst, nct[0:3, :], r=rnct, w=["ncout"], eng='sp')
                yield
            cgv = slice(8, 12)
            for i in range(4):
                wbc = cw[:, cgv, i:i + 1].to_broadcast([128, 4, 128])
                if i == 0:
                    tt('pool', cs[:, cgv, :], xb[:, cgv, 0:128], wbc, ALU.mult, r=["xb", "cw"], w=rcs_v)
                else:
                    tt('pool', ctv, xb[:, cgv, i:i + 128], wbc, ALU.mult, r=["xb", "cw"], w=rctv)
                    tt('pool', cs[:, cgv, :], cs[:, cgv, :], ctv, ALU.add, r=[rcs_v, rctv], w=rcs_v)
            cgq = slice(0, 8)
            for i in range(4):
                wbc = cw[:, cgq, i:i + 1].to_broadcast([128, 8, 128])
                if i == 0:
                    tt('dve', cs[:, cgq, :], xb[:, cgq, 0:128], wbc, ALU.mult, r=["xb", "cw"], w=rcs_qk)
                else:
                    tt('dve', ctmp, xb[:, cgq, i:i + 128], wbc, ALU.mult, r=["xb", "cw"], w=rct)
                    tt('dve', cs[:, cgq, :], cs[:, cgq, :], ctmp, ALU.add, r=[rcs_qk, rct], w=rcs_qk)
                yield
            act(cs, cs, AF.Silu, r=rcs, w=rcs)
            act(sq, cs[:, 0:8, :], AF.Square, r=rcs, w=rsq)
            yield
            for c in range(8):
                mm(ps[c // 4][:, (c % 4) * 128:(c % 4 + 1) * 128], bdonesb[:], sq[:, c, :], True, True,
                   r=["bdonesb", rsq], w=[PSN[c // 4]])
            yield
            for g2 in range(2):
                act(rinv[:, g2 * 4:(g2 + 1) * 4, :], v4(ps[g2][:]), AF.Ln, r=[PSN[g2]], w=rri, bias=NORM_EPS, scale=1.0)
            act(rinv, rinv, AF.Exp, r=rri, w=rri, scale=-0.5)
            yield
            stt('dve', qh, cs[:, 0:4, :], 0.125, rinv[:, 0:4, :], ALU.mult, ALU.mult, r=[rcs, rri], w=rqh)
            tt('dve', khf, cs[:, 4:8, :], rinv[:, 4:8, :], ALU.mult, r=[rcs, rri], w=rkhf)
            cp('act', kh, khf, r=rkhf, w=rkh)
            yield
            for c in range(4):
                tr(ps[2][:, c * 128:(c + 1) * 128], khf[:, c, :], cf(0), r=[rkhf, "CF"], w=[PSN[2]])
                tr(ps[3][:, c * 128:(c + 1) * 128], cs[:, 8 + c, :], cf(0), r=[rcs, "CF"], w=[PSN[3]])
            yield
            tt('dve', v8(kdec), v8(ps[2][:]), sm(10).unsqueeze(2).to_broadcast([128, 8, 64]), ALU.mult, r=[PSN[2], SMALL], w=rkd)
            tt('dve', v8(vbeta), v8(ps[3][:]), sm(3).unsqueeze(2).to_broadcast([128, 8, 64]), ALU.mult, r=[PSN[3], SMALL], w=rvb)
            yield

        merge([g_attn(), g_pre()], list(cfg.get('W1', (1, 1))))
        AR.release(m_long)

        Ttb_g = [AR.get(256, BF16, (4, 128)) for _ in range(2)]
        QKT_g = [AR.get(256, BF16, (4, 128)) for _ in range(2)]
        m2 = AR.mark()
        PB = []
        for par in range(2):
            d_ = dict(heads=[par + 2 * a for a in range(4)], par=par)
            d_['b0'], d_['b1'], d_['b2'] = (2, 3, 4) if par == 0 else (5, 6, 7)
            d_['Bm'], d_['rB'] = AR.get(512, F32, (4, 128))
            d_['dec'], d_['rdec'] = AR.get(512, F32, (4, 128))
            d_['Nf'], d_['rNf'] = AR.get(512, F32, (4, 128))
            d_['N'], d_['rN'] = AR.get(256, BF16, (4, 128))
            d_['Nt'], d_['rNt'] = AR.get(256, BF16, (4, 128))
            d_['Ttb'], d_['rTb'] = Ttb_g[par]
            d_['QKT'], d_['rQ'] = QKT_g[par]
            PB.append(d_)
        def g_w():
            for q in PB:
                g_bc = small[:, 2, :].rearrange("p (a b) -> p a b", b=2)[:, :, q['par']].unsqueeze(2).to_broadcast([128, 4, 128])
                tt('dve', q['Bm'], cf(2).unsqueeze(1).to_broadcast([128, 4, 128]), g_bc, ALU.mult, r=["CF", SMALL], w=q['rB'])
            yield
            for q in PB:
                b0 = q['b0']
                for a, hh in enumerate(q['heads']):
                    outp = ps[b0][:, a * 128:(a + 1) * 128]
                    mm(outp, cf(1), q['Bm'][:, a, :], True, False, r=["CF", q['rB']], w=[PSN[b0]])
                    mm(outp, identb[:], MBb[:], False, True, r=["identb", "MBb"], w=[PSN[b0]])
            yield
            for q in PB:
                b0 = q['b0']
                for a, hh in enumerate(q['heads']):
                    act(q['dec'][:, a, :], ps[b0][:, a * 128:(a + 1) * 128], AF.Exp, r=[PSN[b0], SMALL], w=q['rdec'],
                        bias=small[:, 4, hh:hh + 1], scale=-1.0)
            yield
            for q in PB:
                b1, par = q['b1'], q['par']
                for a, hh in enumerate(q['heads']):
                    pr = hh // 2
                    mm(ps[b1][:, a * 128:(a + 1) * 128], kh[par * 64:(par + 1) * 64, pr, :], kh[par * 64:(par + 1) * 64, pr, :],
                       True, True, r=[rkh], w=[PSN[b1]])
            yield
            for q in PB:
                b1, par = q['b1'], q['par']
                tt('dve', q['Bm'], v4(ps[b1][:]), q['dec'], ALU.mult, r=[PSN[b1], q['rdec']], w=q['rB'])
            yield
            for q in PB:
                par = q['par']
                beta_bc = small[:, 3, :].rearrange("p (a b) -> p a b", b=2)[:, :, par].unsqueeze(2).to_broadcast([128, 4, 128])
                tt('dve', q['Bm'], q['Bm'], beta_bc, ALU.mult, r=[q['rB'], SMALL], w=q['rB'])
                tt('dve', q['Nf'], q['Bm'], cf(4).unsqueeze(1).to_broadcast([128, 4, 128]), ALU.mult, r=[q['rB'], "CF"], w=q['rNf'])
            yield
            for q in PB:
                b0 = q['b0']
                cp('act', q['N'], q['Nf'], r=q['rNf'], w=q['rN'])
                for a in range(4):
                    tr(ps[b0][:, a * 128:(a + 1) * 128], q['Nf'][:, a, :], cf(0), r=[q['rNf'], "CF"], w=[PSN[b0]])
            yield
            for q in PB:
                b0 = q['b0']
                cp('act', q['Nt'], v4(ps[b0][:]), r=[PSN[b0]], w=q['rNt'])
                tt('dve', q['Nf'], v4(ps[b0][:]), cf(0).unsqueeze(1).to_broadcast([128, 4, 128]), ALU.add, r=[PSN[b0], "CF"], w=q['rNf'])
            yield
            for q in PB:
                cp('act', q['Ttb'], q['Nf'], r=q['rNf'], w=q['rTb'])
            yield
            for lev in range(5):
                for q in PB:
                    b0, b1 = q['b0'], q['b1']
                    for a in range(4):
                        mm(ps[b0][:, a * 128:(a + 1) * 128], q['Nt'][:, a, :], q['N'][:, a, :], True, True, r=[q['rNt'], q['rN']], w=[PSN[b0]])
                    if lev < 4:
                        for a in range(4):
                            mm(ps[b1][:, a * 128:(a + 1) * 128], q['N'][:, a, :], q['Nt'][:, a, :], True, True, r=[q['rNt'], q['rN']], w=[PSN[b1]])
                yield
                for q in PB:
                    b0, b1 = q['b0'], q['b1']
                    cp('act', q['N'], v4(ps[b0][:]), r=[PSN[b0]], w=q['rN'])
                    if lev < 4:
                        cp('dve', q['Nt'], v4(ps[b1][:]), r=[PSN[b1]], w=q['rNt'])
                yield
                for q in PB:
                    b2 = q['b2']
                    for a in range(4):
                        mm(ps[b2][:, a * 128:(a + 1) * 128], q['N'][:, a, :], q['Ttb'][:, a, :], True, True, r=[q['rN'], q['rTb']], w=[PSN[b2]])
                yield
                for q in PB:
                    b2 = q['b2']
                    tt('dve', q['Nf'], q['Nf'], v4(ps[b2][:]), ALU.add, r=[q['rNf'], PSN[b2]], w=q['rNf'])
                    cp('act', q['Ttb'], q['Nf'], r=q['rNf'], w=q['rTb'])
            yield
            for q in PB:
                b0, par = q['b0'], q['par']
                for a, hh in enumerate(q['heads']):
                    pr = hh // 2
                    mm(ps[b0][:, a * 128:(a + 1) * 128], qh[par * 64:(par + 1) * 64, pr, :], kh[par * 64:(par + 1) * 64, pr, :],
                       True, True, r=[rqh, rkh], w=[PSN[b0]])
            yield
            for q in PB:
                tt('dve', q['Bm'], v4(ps[q['b0']][:]), q['dec'], ALU.mult, r=[PSN[q['b0']], q['rdec']], w=q['rB'])
            yield
            for q in PB:
                b1 = q['b1']
                for a in range(4):
                    tr(ps[b1][:, a * 128:(a + 1) * 128], q['Bm'][:, a, :], cf(0), r=[q['rB'], "CF"], w=[PSN[b1]])
            yield
            for q in PB:
                cp('act', q['QKT'], v4(ps[q['b1']][:]), r=[PSN[q['b1']]], w=q['rQ'])

            yield

        if next_ac is not None:
            merge([g_w(), next_ac], [1, 1])
        else:
            for _ in g_w():
                pass
        AR.release(m2)

        Zb, rZ = AR.get(256, BF16)
        vn, rvn = AR.get(256, BF16)
        qSe, rqs = AR.get(512)
        og, rog = AR.get(512)
        ztmp, rzt = og, rog
        mix, rmix = AR.top(1024)
        sgb, rsgb = AR.top(1024)
        sg, rsg = AR.top(512)

        def g_rec():
            for c in range(2):
                rr = slice(c * 64, (c + 1) * 64)
                bk = 4 + c * 2
                for pr in range(4):
                    mm(ps[bk][rr, pr * 128:(pr + 1) * 128], kh[:, pr, c * 64:(c + 1) * 64], Sb[:, pr, :], True, True,
                       r=[rkh, "Sb"], w=[PSN[bk]])
                for pr in range(4):
                    mm(ps[bk + 1][rr, pr * 128:(pr + 1) * 128], qh[:, pr, c * 64:(c + 1) * 64], Sb[:, pr, :], True, True,
                       r=[rqh, "Sb"], w=[PSN[bk + 1]])
                yield
                tt('dve', v8(ztmp[rr, :]), v8(ps[bk][rr, :]), small[rr, 11, :].unsqueeze(2).to_broadcast([64, 8, 64]), ALU.mult,
                   r=[PSN[bk], SMALL], w=rzt)
                tt('dve', Zb[rr, :], ztmp[rr, :], vbeta[rr, :], ALU.add, r=[rzt, rvb], w=rZ)
                tt('dve', v8(qSe[rr, :]), v8(ps[bk + 1][rr, :]), small[rr, 7, :].unsqueeze(2).to_broadcast([64, 8, 64]), ALU.mult,
                   r=[PSN[bk + 1], SMALL], w=rqs)
                yield
                for hh in range(8):
                    par, a = hh % 2, hh // 2
                    Ttb, rTb = Ttb_g[par]
                    mm(ps[bk][rr, hh * 64:(hh + 1) * 64], Ttb[rr, a, c * 64:(c + 1) * 64], Zb[rr, hh * 64:(hh + 1) * 64], True, True,
                       r=[rTb, rZ], w=[PSN[bk]])
                yield
                cp('act', vn[rr, :], ps[bk][rr, :], r=[PSN[bk]], w=rvn)
                yield
                for hh in range(8):
                    par, a = hh % 2, hh // 2
                    QKT, rQ = QKT_g[par]
                    mm(ps[bk + 1][rr, hh * 64:(hh + 1) * 64], QKT[rr, a, c * 64:(c + 1) * 64], vn[rr, hh * 64:(hh + 1) * 64], True, True,
                       r=[rQ, rvn], w=[PSN[bk + 1]])
                for pr in range(4):
                    mm(ps[bk][:, pr * 128:(pr + 1) * 128], kdec[rr, pr * 128:(pr + 1) * 128], vn[rr, pr * 128:(pr + 1) * 128], True, True,
                       r=[rkd, rvn], w=[PSN[bk]])
                yield
                S.op('dve', lambda e, c=c: e.tensor_tensor(
                    Sbd[:].rearrange("p a (b d) -> p a b d", b=2), Sbd[:].rearrange("p a (b d) -> p a b d", b=2),
                    small[:, 8 + c, :].rearrange("p (a b) -> p a b", b=2).unsqueeze(3).to_broadcast([128, 4, 2, 64]), ALU.mult),
                    r=["Sbd", SMALL], w=["Sbd"])
                for par in range(2):
                    pp = slice(par * 64, (par + 1) * 64)
                    tt('dve', Sbd[pp, :, par * 64:(par + 1) * 64], Sbd[pp, :, par * 64:(par + 1) * 64],
                       v4(ps[bk][pp, :])[:, :, par * 64:(par + 1) * 64], ALU.add, r=["Sbd", PSN[bk]], w=["Sbd"])
                cp('act', Sb[:], Sbd[:], r=["Sbd"], w=["Sb"])
                yield
                tt('dve', og[rr, :], ps[bk + 1][rr, :], qSe[rr, :], ALU.add, r=[PSN[bk + 1], rqs], w=rog)
                yield
            if sample or t == NT - 1:
                dst = ngs[l, t] if sample else ngp[l]
                for pr in range(4):
                    for par in range(2):
                        dma(dst[pr * 2 + par], Sbd[par * 64:(par + 1) * 64, pr, par * 64:(par + 1) * 64], r=["Sbd"], w=["ngout"], eng='sp')
            osq, ros = qSe, rqs
            act(osq, og, AF.Square, r=rog, w=ros)
            S.op('dve', lambda e: e.tensor_reduce(small[:, 12, :], v8(osq), AX.X, ALU.add), r=ros, w=[SMALL])
            act(sm(12), sm(12), AF.Ln, r=[SMALL], w=[SMALL], bias=NORM_EPS, scale=1.0 / 64.0)
            act(sm(12), sm(12), AF.Exp, r=[SMALL], w=[SMALL], scale=-0.5)
            yield
            tt('dve', v8(osq), v8(og), sm(12).unsqueeze(2).to_broadcast([128, 8, 64]), ALU.mult, r=[rog, SMALL], w=ros)
            tt('dve', v8(osq), v8(osq), gnws[:].unsqueeze(1).to_broadcast([128, 8, 64]), ALU.mult, r=[ros, "gnws"], w=ros)
            tt('dve', osq, osq, zb_s[:], ALU.mult, r=[ros, "zb_s"], w=ros)
            yield
            for c in range(4):
                tr(ps[4][:, c * 128:(c + 1) * 128], osq[:, c * 128:(c + 1) * 128], cf(0), r=[ros, "CF"], w=[PSN[4]])
            cp('act', obgT[:], v4(ps[4][:]), r=[PSN[4]], w=["obgT"])
            yield

        def g_early():
            for nb in range(2):
                for kc in range(4):
                    mm(ps[0][:], oagT[:, kc, :], Wa[:, kc, nb * 512:(nb + 1) * 512], kc == 0, kc == 3, r=["oagT", "Wa"], w=[PSN[0]])
                yield
                for kc in range(8):
                    mm(ps[2][:], hT[:, kc, :], Win[:, kc, C_GA + nb * 512:C_GA + (nb + 1) * 512], kc == 0, kc == 7,
                       r=["Win", hTn], w=[PSN[2]])
                    if kc == 3:
                        yield
                yield
                for kc in range(8):
                    mm(ps[3][:], hT[:, kc, :], Win[:, kc, C_GB + nb * 512:C_GB + (nb + 1) * 512], kc == 0, kc == 7,
                       r=["Win", hTn], w=[PSN[3]])
                    if kc == 3:
                        yield
                yield
                act(sg, ps[2][:], AF.Tanh, r=[PSN[2]], w=rsg, scale=0.5)
                stt('dve', mix[:, nb * 512:(nb + 1) * 512], sg, 1.0, ps[0][:], ALU.add, ALU.mult, r=[PSN[0], rsg], w=rmix)
                act(sgb[:, nb * 512:(nb + 1) * 512], ps[3][:], AF.Tanh, r=[PSN[3]], w=rsgb, scale=0.5)
                yield

        merge([g_rec(), g_early()], list(cfg.get('W2', (1, 1))))
        AR.reset()

        mtmp, rmt = AR.get(512)
        for nb in range(2):
            for kc in range(4):
                mm(ps[1][:], obgT[:, kc, :], Wb[:, kc, nb * 512:(nb + 1) * 512], kc == 0, kc == 3, r=["obgT", "Wb"], w=[PSN[1]])
            stt('dve', mtmp, sgb[:, nb * 512:(nb + 1) * 512], 1.0, ps[1][:], ALU.add, ALU.mult, r=[PSN[1], rsgb], w=rmt)
            tt('dve', mix[:, nb * 512:(nb + 1) * 512], mix[:, nb * 512:(nb + 1) * 512], mtmp, ALU.add, r=[rmix, rmt], w=rmix)
        mixT, rmT = AR.get(512, BF16, (8, 128))
        for half in range(2):
            for c in range(4):
                kc = half * 4 + c
                tr(ps[4 + half][:, c * 128:(c + 1) * 128], mix[:, kc * 128:(kc + 1) * 128], cf(0), r=[rmix, "CF"], w=[PSN[4 + half]])
            if half:
                act(mixT[:, 4:8, :], v4(ps[5][:]), AF.Copy, r=[PSN[5]], w=rmT, scale=0.5)
            else:
                ts('dve', mixT[:, 0:4, :], v4(ps[4][:]), 0.5, None, ALU.mult, None, r=[PSN[4]], w=rmT)
        z, rz = AR.get(1024)
        for nb in range(2):
            for kc in range(8):
                mm(ps[6 + nb][:], mixT[:, kc, :], Wo[:, kc, nb * 512:(nb + 1) * 512], kc == 0, kc == 7, r=[rmT, "Wo"], w=[PSN[6 + nb]])
            stt('dve', z[:, nb * 512:(nb + 1) * 512], h[:, nb * 512:(nb + 1) * 512], ALPHA, ps[6 + nb][:], ALU.mult, ALU.add,
                r=[hn, PSN[6 + nb]], w=rz)
        layer_norm_rows(z, z, rz, rz)
        dma(hout[t * 128:(t + 1) * 128, :], z, r=rz, w=[hout_name], eng='sp')

    def merge(gens, weights):
        live = list(gens)
        while live:
            for g_, w_ in list(zip(live, weights)):
                for _ in range(w_):
                    try:
                        next(g_)
                    except StopIteration:
                        idx = live.index(g_)
                        live.pop(idx)
                        weights = weights[:idx] + weights[idx + 1:]
                        break

    tix = 0
    STOP = cfg.get('STOP')
    for l in range(DEPTH):
        if STOP == 'prologue':
            break
        if not cfg.get('NOLOAD'):
            load_layer(l)
        if STOP == 'load':
            break
        seq = [('p', t) for t in range(NT)] + [('s', t) for t in range(NSB)]
        issue_load(l, seq[0][0], seq[0][1], tix)
        ac_done = False
        for i, (kind, t) in enumerate(seq):
            pf = None
            nac = None
            if i + 1 < len(seq):
                k2, t2 = seq[i + 1]
                pf = (lambda k2=k2, t2=t2, tx=tix + 1, l=l: issue_load(l, k2, t2, tx))
                if k2 == 'p' and 1 <= t2 < NT - NKT and not cfg.get('NOOVL'):
                    nac = ac_stage(l, k2, t2, tix + 1)
            tile(l, kind, t, tix, ac_done=ac_done, prefetch=pf, next_ac=nac)
            ac_done = nac is not None
            tix += 1

    S.emit(nc, es)
    es.close()
    return nc


_IDX = bias_index()


def make_in_maps(cfg, inputs):
    T, NSB, DEPTH, NCORES = cfg['T'], cfg['NSB'], cfg['DEPTH'], cfg['NCORES']
    f = lambda a: np.ascontiguousarray(np.asarray(a, dtype=np.float32))
    x_prompt = f(inputs['x_prompt'])
    x_sample = f(inputs['x_sample'])
    NB = x_prompt.shape[0]
    bc = lambda v, n=128: np.ascontiguousarray(np.broadcast_to(np.asarray(v, np.float32)[None, :], (n, v.shape[-1])))
    rel = f(inputs['rel_bias'])
    btf = np.stack([np.stack([np.stack([rel[l, h][_IDX[pi]] for pi in range(3)]) for h in range(8)]) for l in range(DEPTH)])
    crow = np.stack([np.broadcast_to(rel[l, :, 191][None, :], (128, 8)) for l in range(DEPTH)])
    shared = dict(
        w_in=f(inputs['w_in']), w_a=f(inputs['w_branch_a']), w_b=f(inputs['w_branch_b']), w_o=f(inputs['w_out']),
        convw=np.ascontiguousarray(f(inputs['conv_w']).transpose(0, 2, 1)),
        ln0=np.stack([bc(f(inputs['ln0_g'])), bc(f(inputs['ln0_b']))]),
        lnl=np.stack([np.stack([bc(f(inputs['ln_g'])[l]), bc(f(inputs['ln_b'])[l])]) for l in range(DEPTH)]),
        alog=np.stack([bc(f(inputs['a_log'])[l]) for l in range(DEPTH)]),
        dtb=np.stack([bc(f(inputs['dt_bias'])[l]) for l in range(DEPTH)]),
        gnw=np.stack([bc(f(inputs['gdn_norm_w'])[l]) for l in range(DEPTH)]),
        bt=f(btf), crow=f(crow), consts=host_consts(),
    )
    in_maps = []
    for c in range(NCORES):
        b = c % NB
        sbs = list(range(c * NSB, (c + 1) * NSB))
        xs = np.zeros((NSB, 128, D), np.float32)
        xs[:, 0:32, :] = x_sample[sbs]
        m = dict(shared)
        m['xp'] = x_prompt[b]
        m['xs'] = xs.reshape(NSB * 128, D)
        m['ck'] = f(inputs['cache_attn_k'][:, sbs]).reshape(DEPTH, NSB, 512, 512)
        m['cv'] = f(inputs['cache_attn_v'][:, sbs]).reshape(DEPTH, NSB, 512, 512)
        m['sconv'] = f(inputs['state_conv'][:, sbs])
        m['sgdn'] = f(inputs['state_gdn'][:, sbs])
        in_maps.append(m)
    return in_maps


def gather(cfg, res, NB):
    T, NSB, DEPTH, NCORES = cfg['T'], cfg['NSB'], cfg['DEPTH'], cfg['NCORES']
    KEEP = min(512, T)
    R = res
    yp = np.stack([R[b]['yp'] for b in range(NB)])
    ys = np.concatenate([R[c]['ys'].reshape(NSB, 128, D)[:, 0:32] for c in range(NCORES)], axis=0)
    nkp = np.stack([R[b]['nkp'] for b in range(NB)], axis=1).reshape(DEPTH, NB, KEEP, 8, 64)
    nvp = np.stack([R[b]['nvp'] for b in range(NB)], axis=1).reshape(DEPTH, NB, KEEP, 8, 64)
    ncp = np.stack([R[b]['ncp'] for b in range(NB)], axis=1)
    ngp = np.stack([R[b]['ngp'] for b in range(NB)], axis=1)
    nks = np.concatenate([R[c]['nks'] for c in range(NCORES)], axis=1).reshape(DEPTH, NCORES * NSB, 32, 8, 64)
    nvs = np.concatenate([R[c]['nvs'] for c in range(NCORES)], axis=1).reshape(DEPTH, NCORES * NSB, 32, 8, 64)
    ncs = np.concatenate([R[c]['ncs'] for c in range(NCORES)], axis=1)
    ngs = np.concatenate([R[c]['ngs'] for c in range(NCORES)], axis=1)
    return tuple(np.ascontiguousarray(a, dtype=np.float32) for a in (yp, ys, nkp, nvp, ncp, ngp, nks, nvs, ncs, ngs))


_NC_CACHE = {}


def kernel(**inputs):
    cfg = dict(CFG)
    key = tuple(sorted(cfg.items()))
    if key not in _NC_CACHE:
        _NC_CACHE[key] = build(cfg)
    nc = _NC_CACHE[key]
    in_maps = make_in_maps(cfg, inputs)
    res = run_bass_kernel_spmd(nc, in_maps, core_ids=list(range(cfg['NCORES'])))
    return gather(cfg, res.results, np.asarray(inputs['x_prompt']).shape[0])
```

```python
import contextlib
import numpy as np
import concourse.bass as bass
import concourse.mybir as mybir
from concourse.bass_utils import run_bass_kernel_spmd

F32 = mybir.dt.float32
BF16 = mybir.dt.bfloat16
AF = mybir.ActivationFunctionType
ALU = mybir.AluOpType
AX = mybir.AxisListType

D = 1024
IN_DIM = 6160
ALPHA = 4.0 ** 0.25
LN_EPS = 1e-5
NORM_EPS = 1e-6
BIG = 30000.0
C_QA, C_KA, C_VA, C_ZA, C_QKVB, C_ZB, C_AB, C_GA, C_GB = 0, 512, 1024, 1536, 2048, 3584, 4096, 4112, 5136

CFG = dict(T=8192, NSB=4, DEPTH=2, NCORES=8)


class Sched:
    ENGS = ['pe', 'act', 'dve', 'pool', 'sp']
    KDMA = 8
    EPOCH = 4000
    NEP = 12

    def __init__(self):
        self.ops = {e: [] for e in self.ENGS}
        self.res = {}

    def op(self, eng, fn, r=(), w=(), dma=False):
        idx = len(self.ops[eng])
        deps = set()
        rl = _flat(r)
        wl = _flat(w)
        for n in rl:
            st = self.res.setdefault(n, [None, {}, []])
            if st[0] is not None:
                deps.add(st[0])
            if n.startswith('ps'):
                for e_, i_ in st[1].items():
                    if e_ != eng:
                        deps.add((e_, i_))
        for n in wl:
            st = self.res.setdefault(n, [None, {}, []])
            if st[0] is not None:
                deps.add(st[0])
            for e_, i_ in st[1].items():
                deps.add((e_, i_))
            for rd in st[2]:
                deps.add(rd)
        for n in rl:
            if dma:
                self.res[n][2].append((eng, idx))
            else:
                self.res[n][1][eng] = idx
        for n in wl:
            self.res[n] = [(eng, idx), {}, []]
        deps.discard((eng, idx))
        self.ops[eng].append(dict(fn=fn, deps=deps, dma=dma, sig=False))

    def finalize(self):
        for e in self.ENGS:
            for o in self.ops[e]:
                for (d, i) in o['deps']:
                    if d == 'pe' and e == 'pe':
                        continue
                    self.ops[d][i]['sig'] = True
        for e in self.ENGS:
            cnt = 0
            nd = 0
            for o in self.ops[e]:
                if o['dma']:
                    o['dsem'] = nd % self.KDMA
                    o['dval'] = 16 * (nd // self.KDMA + 1)
                    nd += 1
                elif o['sig']:
                    o['ep'] = cnt // self.EPOCH
                    o['sval'] = cnt % self.EPOCH + 1
                    cnt += 1
            assert cnt <= self.EPOCH * self.NEP, (e, cnt)

    def emit(self, nc, stack):
        self.finalize()
        sems = {e: [stack.enter_context(nc.semaphore(f"s_{e}_{k}")) for k in range(self.NEP)] for e in self.ENGS}
        dsems = {e: [stack.enter_context(nc.semaphore(f"d_{e}_{k}")) for k in range(self.KDMA)] for e in self.ENGS}
        block = stack.enter_context(nc.Block())
        ops = self.ops
        KD = self.KDMA

        def body(ename):
            def run(eng):
                waited = {}

                def wait(key, sem, val):
                    if waited.get(key, 0) >= val:
                        return
                    waited[key] = val
                    eng.wait_ge(sem, val)
                dcount = {}
                for o in ops[ename]:
                    for (d, i) in sorted(o['deps']):
                        po = ops[d][i]
                        if po['dma']:
                            wait(('d', d, po['dsem']), dsems[d][po['dsem']], po['dval'])
                        else:
                            if d == 'pe' and ename == 'pe':
                                continue
                            wait(('s', d, po['ep']), sems[d][po['ep']], po['sval'])
                    if o['dma']:
                        if o['dval'] > 16:
                            wait(('d', ename, o['dsem']), dsems[ename][o['dsem']], o['dval'] - 16)
                        inst = o['fn'](eng)
                        inst.then_inc(dsems[ename][o['dsem']], 16)
                        dcount[o['dsem']] = o['dval']
                    else:
                        inst = o['fn'](eng)
                        if o['sig']:
                            inst.then_inc(sems[ename][o['ep']], 1)
                for k, v in sorted(dcount.items()):
                    wait(('d', ename, k), dsems[ename][k], v)
            return run

        block.tensor(body('pe'))
        block.scalar(body('act'))
        block.vector(body('dve'))
        block.gpsimd(body('pool'))
        block.sync(body('sp'))


def _flat(x):
    out = []
    for a in x:
        if isinstance(a, (list, tuple)):
            out.extend(_flat(a))
        else:
            out.append(a)
    return out


def host_consts():
    i = np.arange(128)[:, None]
    j = np.arange(128)[None, :]
    same = (i // 64) == (j // 64)
    c = np.zeros((12, 128, 128), np.float32)
    c[0] = np.eye(128)
    c[1] = 1.0
    c[2] = (same & (i <= j))
    c[3] = np.where(same & (j <= i), 0.0, BIG)
    c[4] = np.where(same & (j < i), -1.0, 0.0)
    c[5] = same
    c[6] = np.where(i >= 32, -BIG, 0.0) * np.ones((1, 128))
    c[7][:, 0] = (np.arange(128) < 32)
    c[7][:, 1] = 1.0
    c[8] = np.where((i < 64) & (j >= 64), -BIG, 0.0)
    c[9] = np.where((i >= 64) & (j < 64), -BIG, 0.0)
    c[10] = np.where(same & (j <= i), 1.0, 0.0)
    return np.ascontiguousarray(c.transpose(1, 0, 2).reshape(128, 12 * 128))


def bias_index():
    kk = np.arange(128)[:, None]
    qq = np.arange(128)[None, :]
    out = []
    for p in (0, 3, 4):
        out.append(np.clip(qq - kk + (4 - p) * 128, -63, 128) + 63)
    return out


def build(cfg):
    T, NSB, DEPTH = cfg['T'], cfg['NSB'], cfg['DEPTH']
    NT = T // 128
    KEEP = min(512, T)
    NKT = KEEP // 128
    nc = bass.Bass("TRN2", target_bir_lowering=False)
    S = Sched()
    es = contextlib.ExitStack()

    def din(name, shape):
        return nc.dram_tensor(name, list(shape), F32, kind="ExternalInput").ap()

    def dout(name, shape):
        return nc.dram_tensor(name, list(shape), F32, kind="ExternalOutput").ap()

    def dint(name, shape):
        return nc.dram_tensor(name, list(shape), F32, kind="Internal").ap()

    xp = din("xp", [T, D])
    xs = din("xs", [NSB * 128, D])
    ck = din("ck", [DEPTH, NSB, 512, 512])
    cv = din("cv", [DEPTH, NSB, 512, 512])
    sconv = din("sconv", [DEPTH, NSB, 3, 1536])
    sgdn = din("sgdn", [DEPTH, NSB, 8, 64, 64])
    w_in = din("w_in", [DEPTH, D, IN_DIM])
    w_a = din("w_a", [DEPTH, 512, D])
    w_b = din("w_b", [DEPTH, 512, D])
    w_o = din("w_o", [DEPTH, D, D])
    convw = din("convw", [DEPTH, 1536, 4])
    ln0 = din("ln0", [2, 128, D])
    lnl = din("lnl", [DEPTH, 2, 128, D])
    alog = din("alog", [DEPTH, 128, 8])
    dtb = din("dtb", [DEPTH, 128, 8])
    gnw = din("gnw", [DEPTH, 128, 64])
    bt = din("bt", [DEPTH, 8, 3, 128, 128])
    crow = din("crow", [DEPTH, 128, 8])
    consts = din("consts", [128, 12 * 128])

    yp = dout("yp", [T, D])
    ys = dout("ys", [NSB * 128, D])
    nkp = dout("nkp", [DEPTH, KEEP, 512])
    nvp = dout("nvp", [DEPTH, KEEP, 512])
    ncp = dout("ncp", [DEPTH, 3, 1536])
    ngp = dout("ngp", [DEPTH, 8, 64, 64])
    nks = dout("nks", [DEPTH, NSB, 32, 512])
    nvs = dout("nvs", [DEPTH, NSB, 32, 512])
    ncs = dout("ncs", [DEPTH, NSB, 3, 1536])
    ngs = dout("ngs", [DEPTH, NSB, 8, 64, 64])
    hbuf_p = [dint("h0p", [T, D]), dint("h1p", [T, D])]
    hbuf_s = [dint("h0s", [NSB * 128, D]), dint("h1s", [NSB * 128, D])]

    def sb(name, shape, dt=F32):
        return es.enter_context(nc.sbuf_tensor(name, list(shape), dt))

    Win = sb("Win", [128, 8, IN_DIM], BF16)
    Wa = sb("Wa", [128, 4, D], BF16)
    Wb = sb("Wb", [128, 4, D], BF16)
    Wo = sb("Wo", [128, 8, D], BF16)
    CF = sb("CF", [128, 6, 128])
    identb = sb("identb", [128, 128], BF16)
    bdonesb = sb("bdonesb", [128, 128], BF16)
    lnG = sb("lnG", [128, D])
    lnB = sb("lnB", [128, D])
    hx = [sb("hx0", [128, D]), sb("hx1", [128, D])]
    hTs = [sb("hT0", [128, 8, 128], BF16), sb("hT1", [128, 8, 128], BF16)]
    qT = sb("qT", [128, 4, 128], BF16)
    kring = sb("kring", [128, 4, 640], BF16)
    vring = sb("vring", [128, 5, 8, 65], BF16)
    BT = sb("BT", [128, 8, 2, 128], BF16)
    maskp0b = sb("maskp0b", [128, 128], BF16)
    maskSb = sb("maskSb", [128, 128], BF16)
    MBb = sb("MBb", [128, 128], BF16)
    crows = sb("crows", [128, 8])
    za_s = sb("za_s", [128, 512], BF16)
    zb_s = sb("zb_s", [128, 512], BF16)
    xb = sb("xb", [128, 12, 131])
    cw = sb("cw", [128, 12, 4])
    Sbd = sb("Sbd", [128, 4, 128])
    Sb = sb("Sb", [128, 4, 128], BF16)
    oagT = sb("oagT", [128, 4, 128], BF16)
    obgT = sb("obgT", [128, 4, 128], BF16)
    small = sb("small", [128, 24, 8])
    negA = sb("negA", [128, 8])
    dtbs = sb("dtbs", [128, 8])
    gnws = sb("gnws", [128, 64])
    ARW = 6400
    arena = sb("arena", [128, ARW])
    arena_b = arena.bitcast(BF16)

    class Arena:
        def __init__(self):
            self.off = 0

        def reset(self):
            self.off = 0
            self.topoff = ARW

        def mark(self):
            return self.off

        def top(self, words):
            self.topoff = getattr(self, 'topoff', ARW) - words
            o = self.topoff
            res = [f"ar{g}" for g in range(o // 256, (o + words - 1) // 256 + 1)]
            return arena[:, o:o + words], res

        def release(self, m):
            self.off = m

        def get(self, words, dt=F32, shape=None):
            o = self.off
            self.off += words
            assert self.off <= ARW, self.off
            res = [f"ar{g}" for g in range(o // 256, (o + words - 1) // 256 + 1)]
            if dt == F32:
                ap = arena[:, o:o + words]
            else:
                ap = arena_b[:, 2 * o:2 * o + 2 * words]
            if shape is not None and len(shape) == 2:
                ap = ap.rearrange("p (a b) -> p a b", a=shape[0])
            return ap, res
    AR = Arena()

    ps = [es.enter_context(nc.psum_tensor(f"ps{i}", [128, 512], F32)) for i in range(8)]
    PSN = [f"ps{i}" for i in range(8)]

    def cf(i):
        return CF[:, i, :]

    def dma(out, in_, r, w, eng='sp', slow=False):
        if slow:
            S.op(eng, lambda e: e.dma_start(out=out, in_=in_, allow_slow_non_contiguous=True), r=r, w=w, dma=True)
        else:
            S.op(eng, lambda e: e.dma_start(out=out, in_=in_), r=r, w=w, dma=True)

    def mm(out, lhsT, rhs, start, stop, r, w, tp=None):
        if tp is None:
            S.op('pe', lambda e: e.matmul(out, lhsT, rhs, start=start, stop=stop), r=r, w=w)
        else:
            S.op('pe', lambda e: e.matmul(out, lhsT, rhs, start=start, stop=stop, tile_position=tp), r=r, w=w)

    def tr(out, in_, ident, r, w):
        S.op('pe', lambda e: e.transpose(out, in_, ident), r=r, w=w)

    def act(out, in_, func, r, w, bias=None, scale=None, accum=None):
        kw = {}
        if bias is not None:
            kw['bias'] = bias
        if scale is not None:
            kw['scale'] = scale
        if accum is not None:
            kw['accum_out'] = accum
        S.op('act', lambda e: e.activation(out, in_, func, **kw), r=r, w=w)

    def ts(eng, out, in0, s1, s2, op0, op1, r, w):
        if op1 is None:
            S.op(eng, lambda e: e.tensor_scalar(out, in0, s1, None, op0), r=r, w=w)
        else:
            S.op(eng, lambda e: e.tensor_scalar(out, in0, s1, s2, op0, op1), r=r, w=w)

    def tt(eng, out, in0, in1, op, r, w):
        S.op(eng, lambda e: e.tensor_tensor(out, in0, in1, op), r=r, w=w)

    def stt(eng, out, in0, sc, in1, op0, op1, r, w):
        S.op(eng, lambda e: e.scalar_tensor_tensor(out, in0, sc, in1, op0, op1), r=r, w=w)

    def cp(eng, out, in_, r, w):
        if eng == 'act':
            S.op('act', lambda e: e.activation(out, in_, AF.Copy), r=r, w=w)
        else:
            S.op(eng, lambda e: e.tensor_copy(out, in_), r=r, w=w)

    c3 = consts.rearrange("p (a b) -> p a b", a=12)
    dma(CF[:, 0:5, :], c3[:, 0:5, :], r=[], w=["CF"])
    dma(CF[:, 5, :], c3[:, 7, :], r=[], w=["CF"])
    VALID_S = CF[:, 5, 0:1]
    VALID_P = CF[:, 5, 1:2]
    AR.reset()
    stg0, rs0 = AR.get(512, F32, (4, 128))
    dma(stg0[:, 0, :], c3[:, 5, :], r=[], w=rs0)
    dma(stg0[:, 1, :], c3[:, 6, :], r=[], w=rs0)
    dma(stg0[:, 2, :], c3[:, 8, :], r=[], w=rs0)
    dma(stg0[:, 3, :], c3[:, 9, :], r=[], w=rs0)
    maskp4f = sb("maskp4f", [128, 128])
    cp('dve', identb[:], cf(0), r=["CF"], w=["identb"])
    cp('dve', MBb[:], cf(3), r=["CF"], w=["MBb"])
    cp('dve', bdonesb[:], stg0[:, 0, :], r=rs0, w=["bdonesb"])
    cp('dve', maskSb[:], stg0[:, 1, :], r=rs0, w=["maskSb"])
    cp('dve', maskp0b[:], stg0[:, 2, :], r=rs0, w=["maskp0b"])
    cp('dve', maskp4f[:], stg0[:, 3, :], r=rs0, w=["maskp4f"])
    S.op('dve', lambda e: e.memset(vring[:], 1.0), r=[], w=["vring"])

    def layer_norm_rows(src, dst, rs, rd):
        st = small[:, 16:22, :].rearrange("p a b -> p (a b)")
        stats = st[:, 0:12]
        mv = st[:, 12:14]
        rstd = st[:, 14:15]
        xr = src.rearrange("p (c f) -> p c f", c=2)
        S.op('dve', lambda e: e.bn_stats(stats[:, 0:6], xr[:, 0, :]), r=[rs], w=["lnst"])
        S.op('dve', lambda e: e.bn_stats(stats[:, 6:12], xr[:, 1, :]), r=[rs], w=["lnst"])
        S.op('dve', lambda e: e.bn_aggr(mv, stats.rearrange("p (c s) -> p c s", c=2)), r=["lnst"], w=["lnst"])
        act(rstd, mv[:, 1:2], AF.Ln, r=["lnst"], w=["lnst"], bias=LN_EPS, scale=1.0)
        act(rstd, rstd, AF.Exp, r=["lnst"], w=["lnst"], scale=-0.5)
        stt('dve', dst, src, mv[:, 0:1], lnG[:], ALU.subtract, ALU.mult, r=[rs, "lnst", "lnG"], w=[rd])
        stt('dve', dst, dst, rstd, lnB[:], ALU.mult, ALU.add, r=[rd, "lnst", "lnB"], w=[rd])

    dma(lnG[:], ln0[0], r=[], w=["lnG"])
    dma(lnB[:], ln0[1], r=[], w=["lnB"])
    pbufs = [AR.get(1024) for _ in range(4)]
    tiles = [(xp, hbuf_p[0], t, "h0p") for t in range(NT)] + [(xs, hbuf_s[0], t, "h0s") for t in range(NSB)]
    for i, (srcd, dstd, t, rn) in enumerate(tiles):
        b, rb = pbufs[i % 4]
        dma(b, srcd[t * 128:(t + 1) * 128, :], r=[], w=rb)
        layer_norm_rows(b, b, rb, rb)
        dma(dstd[t * 128:(t + 1) * 128, :], b, r=rb, w=[f"{rn}_{t}"], eng='sp')

    cast_rr = [0]

    def cast(out, in_, r, w):
        k = cast_rr[0] % 2
        cast_rr[0] += 1
        cp(['act', 'dve'][k], out, in_, r, w)

    def load_layer(l):
        AR.reset()
        stg = [AR.get(1540) for _ in range(4)]
        n = 0
        for kc in range(8):
            for cb in range(4):
                s_, r_ = stg[n % 4]
                n += 1
                dma(s_, w_in[l, kc * 128:(kc + 1) * 128, cb * 1540:(cb + 1) * 1540], r=[], w=r_)
                cast(Win[:, kc, cb * 1540:(cb + 1) * 1540], s_, r=r_, w=["Win"])
        for (wsrc, wdst, wn, nk) in ((w_a, Wa, "Wa", 4), (w_b, Wb, "Wb", 4), (w_o, Wo, "Wo", 8)):
            for kc in range(nk):
                s_, r_ = stg[n % 4]
                n += 1
                dma(s_[:, 0:1024], wsrc[l, kc * 128:(kc + 1) * 128, :], r=[], w=r_)
                cast(wdst[:, kc, :], s_[:, 0:1024], r=r_, w=[wn])
        dma(lnG[:], lnl[l, 0], r=[], w=["lnG"])
        dma(lnB[:], lnl[l, 1], r=[], w=["lnB"])
        dma(cw[:], convw[l].rearrange("(c p) i -> p c i", p=128), r=[], w=["cw"])
        dma(dtbs[:], dtb[l], r=[], w=["dtbs"])
        dma(gnws[:], gnw[l], r=[], w=["gnws"])
        dma(crows[:], crow[l], r=[], w=["crows"])
        dma(negA[:], alog[l], r=[], w=["negA"])
        act(negA[:], negA[:], AF.Exp, r=["negA"], w=["negA"])
        ts('dve', negA[:], negA[:], -1.0, None, ALU.mult, None, r=["negA"], w=["negA"])
        for h in range(8):
            for pi in (1, 2):
                s_, r_ = stg[n % 4]
                n += 1
                dma(s_[:, 0:128], bt[l, h, pi], r=[], w=r_)
                if pi == 1:
                    ts('dve', BT[:, h, 0, :], s_[:, 0:128], crows[:, h:h + 1], None, ALU.subtract, None, r=[r_, "crows"], w=["BT"])
                else:
                    stt('dve', BT[:, h, 1, :], s_[:, 0:128], crows[:, h:h + 1], maskp4f[:], ALU.subtract, ALU.add,
                        r=[r_, "crows", "maskp4f"], w=["BT"])

    def issue_load(l, kind, t, tix):
        hin_ = (hbuf_s if kind == 's' else hbuf_p)[l]
        dma(hx[tix % 2][:], hin_[t * 128:(t + 1) * 128, :], r=[f"h{l}{kind}_{t}"], w=[f"hx{tix % 2}"])

    def ac_stage(l, kind, t, tix):
        sample = (kind == 's')
        h = hx[tix % 2]
        hn = f"hx{tix % 2}"
        hT = hTs[tix % 2]
        hTn = f"hT{tix % 2}"
        first = sample or t == 0
        slot = 4 if sample else t % 5
        v4 = lambda ap: ap.rearrange("p (a b) -> p a b", a=4)
        v8 = lambda ap: ap.rearrange("p (h d) -> p h d", h=8)
        for half in range(2):
            for c in range(4):
                kc = half * 4 + c
                tr(ps[half][:, c * 128:(c + 1) * 128], h[:, kc * 128:(kc + 1) * 128], cf(0), r=[hn, "CF"], w=[PSN[half]])
            cp('act' if half else 'dve', hT[:, half * 4:half * 4 + 4, :], v4(ps[half][:]), r=[PSN[half]], w=[hTn])
            yield

        def proj_fm(col0, nch, bank):
            for c in range(nch):
                for kc in range(8):
                    mm(ps[bank][:, c * 128:(c + 1) * 128], Win[:, kc, col0 + c * 128:col0 + (c + 1) * 128], hT[:, kc, :],
                       kc == 0, kc == 7, r=["Win", hTn], w=[PSN[bank]])
                yield

        def proj_tm(col0, ncol, bank):
            for kc in range(8):
                mm(ps[bank][:, 0:ncol], hT[:, kc, :], Win[:, kc, col0:col0 + ncol], kc == 0, kc == 7,
                   r=["Win", hTn], w=[PSN[bank]])
                if kc == 3:
                    yield
            yield

        if sample:
            AR.reset()
            sbufs = [AR.get(512) for _ in range(4)]
            for blk in range(4):
                stg, rstg = sbufs[(2 * blk) % 4]
                dma(stg, ck[l, t, blk * 128:(blk + 1) * 128, :], r=[], w=rstg)
                for c in range(4):
                    tr(ps[2][:, c * 128:(c + 1) * 128], stg[:, c * 128:(c + 1) * 128], cf(0), r=[rstg, "CF"], w=[PSN[2]])
                cp('dve', kring[:, :, blk * 128:(blk + 1) * 128], v4(ps[2][:]), r=[PSN[2]], w=["kring"])
                stg2, rstg2 = sbufs[(2 * blk + 1) % 4]
                dma(stg2, cv[l, t, blk * 128:(blk + 1) * 128, :], r=[], w=rstg2)
                cp('act', vring[:, blk, :, 0:64], v8(stg2), r=rstg2, w=["vring"])
            AR.reset()
            sct, rsct = AR.get(1536)
            dma(sct[0:3, :], sconv[l, t], r=[], w=rsct)
            for c in range(12):
                tr(ps[3][:, c * 3:(c + 1) * 3], sct[0:3, c * 128:(c + 1) * 128], CF[0:3, 0, 0:3], r=[rsct, "CF"], w=[PSN[3]])
            cp('dve', xb[:, :, 128:131], ps[3][:, 0:36].rearrange("p (c i) -> p c i", c=12), r=[PSN[3]], w=["xb"])
            AR.reset()
            S.op('dve', lambda e: e.memset(Sbd[:], 0.0), r=[], w=["Sbd"])
            for pr in range(4):
                for par in range(2):
                    dma(Sbd[par * 64:(par + 1) * 64, pr, par * 64:(par + 1) * 64], sgdn[l, t, pr * 2 + par], r=[], w=["Sbd"])
            cp('act', Sb[:], Sbd[:], r=["Sbd"], w=["Sb"])
        elif first:
            S.op('dve', lambda e: e.memset(xb[:, :, 128:131], 0.0), r=[], w=["xb"])
            S.op('dve', lambda e: e.memset(Sbd[:], 0.0), r=[], w=["Sbd"])
            S.op('dve', lambda e: e.memset(Sb[:], 0.0), r=[], w=["Sb"])

        yield from proj_fm(C_QA, 4, 0)
        ts('dve', qT[:], v4(ps[0][:]), 0.125, None, ALU.mult, None, r=[PSN[0]], w=["qT"])
        yield from proj_fm(C_KA, 4, 1)
        cp('act', kring[:, :, slot * 128:(slot + 1) * 128], v4(ps[1][:]), r=[PSN[1]], w=["kring"])
        yield from proj_tm(C_VA, 512, 0)
        cp('dve', vring[:, slot, :, 0:64], v8(ps[0][:]), r=[PSN[0]], w=["vring"])
        want_kv = sample or (t >= NT - NKT)
        if want_kv:
            AR.reset()
            vst, rv = AR.get(512)
            cp('act', vst, ps[0][:], r=[PSN[0]], w=rv)
            if sample:
                dma(nvs[l, t], vst[0:32, :], r=rv, w=["nvs"], eng='sp')
            else:
                o0 = (t - (NT - NKT)) * 128
                dma(nvp[l, o0:o0 + 128, :], vst, r=rv, w=["nvp"], eng='sp')
        yield from proj_tm(C_ZA, 512, 1)
        act(za_s[:], ps[1][:], AF.Silu, r=[PSN[1]], w=["za_s"])
        yield
        if want_kv:
            yield from proj_tm(C_KA, 512, 0)
            kst, rk = AR.get(512)
            cp('act', kst, ps[0][:], r=[PSN[0]], w=rk)
            if sample:
                dma(nks[l, t], kst[0:32, :], r=rk, w=["nks"], eng='sp')
            else:
                o0 = (t - (NT - NKT)) * 128
                dma(nkp[l, o0:o0 + 128, :], kst, r=rk, w=["nkp"], eng='sp')
            yield

    def tile(l, kind, t, tix, ac_done=False, prefetch=None, next_ac=None):
        sample = (kind == 's')
        last_layer = (l == DEPTH - 1)
        if last_layer:
            hout = ys if sample else yp
            hout_name = f"y{kind}_{t}"
        else:
            hout = (hbuf_s if sample else hbuf_p)[l + 1]
            hout_name = f"h{l + 1}{kind}_{t}"
        h = hx[tix % 2]
        hn = f"hx{tix % 2}"
        hT = hTs[tix % 2]
        hTn = f"hT{tix % 2}"
        validcol = VALID_S if sample else VALID_P
        SMALL = "small"
        sm = lambda i: small[:, i, :]
        v4 = lambda ap: ap.rearrange("p (a b) -> p a b", a=4)
        v8 = lambda ap: ap.rearrange("p (h d) -> p h d", h=8)

        def proj_tm(col0, ncol, bank):
            for kc in range(8):
                mm(ps[bank][:, 0:ncol], hT[:, kc, :], Win[:, kc, col0:col0 + ncol], kc == 0, kc == 7,
                   r=["Win", hTn], w=[PSN[bank]])

        if not ac_done:
            for _ in ac_stage(l, kind, t, tix):
                pass
        if prefetch is not None:
            prefetch()
        AR.reset()

        AR.reset()
        qh, rqh = AR.get(256, BF16, (4, 128))
        kh, rkh = AR.get(256, BF16, (4, 128))
        kdec, rkd = AR.get(256, BF16)
        vbeta, rvb = AR.get(512)
        m_long = AR.mark()
        PTs = [AR.get(320, BF16), AR.get(320, BF16)]
        oag, roag = AR.get(512)
        rden, rrden = AR.get(8)
        cs, rcs = AR.get(1536, F32, (12, 128))
        rcs_qk, rcs_v = rcs[:4], rcs[4:]
        ctmp, rct = AR.get(1024, F32, (8, 128))
        AR.off -= 1024
        sq, rsq = AR.get(512, BF16, (8, 128))
        AR.off -= 512
        rinv, rri = AR.get(1024, F32, (8, 128))
        ctv, rctv = AR.get(512, F32, (4, 128))
        khf, rkhf = AR.get(512, F32, (4, 128))
        if sample:
            plist = [0, 1, 2, 3, 4]
        else:
            plist = [p for p in range(5) if t - 4 + p >= 0]

        def g_attn():
            for hh in range(8):
                pr, par = hh // 2, hh % 2
                bA, bB = (4, 5) if par == 0 else (6, 7)
                PT, rPT = PTs[par]
                for p in plist:
                    sl = p if sample else (t - 4 + p) % 5
                    bank = bA if p < 4 else bB
                    col = (p % 4) * 128
                    outp = ps[bank][:, col:col + 128]
                    extra = []
                    if p == 0:
                        extra.append((maskp0b[:], "maskp0b"))
                    elif p == 3:
                        extra.append((BT[:, hh, 0, :], "BT"))
                    elif p == 4:
                        extra.append((BT[:, hh, 1, :], "BT"))
                        if sample:
                            extra.append((maskSb[:], "maskSb"))
                    mm(outp, kring[par * 64:(par + 1) * 64, pr, sl * 128:(sl + 1) * 128], qT[par * 64:(par + 1) * 64, pr, :],
                       True, len(extra) == 0, r=["kring", "qT"], w=[PSN[bank]])
                    for ei, (eap, en) in enumerate(extra):
                        mm(outp, identb[:], eap, False, ei == len(extra) - 1, r=["identb", en], w=[PSN[bank]])
                yield
                p0 = plist[0]
                if p0 < 4:
                    act(PT[:, p0 * 128:512], ps[bA][:, p0 * 128:512], AF.Exp, r=[PSN[bA]], w=rPT)
                act(PT[:, 512:640], ps[bB][:, 0:128], AF.Exp, r=[PSN[bB]], w=rPT)
                yield
                ob = 5 if hh < 4 else 7
                j = hh % 4
                for p in plist:
                    sl = p if sample else (t - 4 + p) % 5
                    mm(ps[ob][:, 128 + j * 65:128 + (j + 1) * 65], PT[:, p * 128:(p + 1) * 128], vring[:, sl, hh, :],
                       p == plist[0], p == plist[-1], r=[rPT, "vring"], w=[PSN[ob]])
                yield
                if j == 3:
                    g4 = hh // 4
                    S.op('dve', lambda e, ob=ob, g4=g4: e.reciprocal(
                        rden[:, g4 * 4:(g4 + 1) * 4], ps[ob][:, 128:388].rearrange("p (h d) -> p h d", h=4)[:, :, 64]),
                        r=[PSN[ob]], w=rrden)
                    og4 = oag[:, g4 * 256:(g4 + 1) * 256]
                    tt('dve', og4.rearrange("p (h d) -> p h d", h=4), ps[ob][:, 128:388].rearrange("p (h d) -> p h d", h=4)[:, :, 0:64],
                       rden[:, g4 * 4:(g4 + 1) * 4].unsqueeze(2).to_broadcast([128, 4, 64]), ALU.mult, r=[PSN[ob], rrden], w=roag)
                    tt('dve', og4, og4, za_s[:, g4 * 256:(g4 + 1) * 256], ALU.mult, r=[roag, "za_s"], w=roag)
                    yield
            for c in range(4):
                tr(ps[4][:, c * 128:(c + 1) * 128], oag[:, c * 128:(c + 1) * 128], cf(0), r=[roag, "CF"], w=[PSN[4]])
            cp('act', oagT[:], v4(ps[4][:]), r=[PSN[4]], w=["oagT"])
            yield

        def g_pre():
            proj_tm(C_AB, 16, 0)
            cp('dve', small[:, 0:2, :], ps[0][:, 0:16].rearrange("p (a b) -> p a b", a=2), r=[PSN[0]], w=[SMALL])
            yield
            tt('dve', sm(2), sm(0), dtbs[:], ALU.add, r=[SMALL, "dtbs"], w=[SMALL])
            act(sm(2), sm(2), AF.Exp, r=[SMALL], w=[SMALL])
            act(sm(2), sm(2), AF.Ln, r=[SMALL], w=[SMALL], bias=1.0)
            tt('dve', sm(2), sm(2), negA[:], ALU.mult, r=[SMALL, "negA"], w=[SMALL])
            if sample:
                ts('dve', sm(2), sm(2), validcol, None, ALU.mult, None, r=[SMALL, "CF"], w=[SMALL])
            act(sm(3), sm(1), AF.Tanh, r=[SMALL], w=[SMALL], scale=0.5)
            ts('dve', sm(3), sm(3), 0.5, 0.5, ALU.mult, ALU.add, r=[SMALL], w=[SMALL])
            if sample:
                ts('dve', sm(3), sm(3), validcol, None, ALU.mult, None, r=[SMALL, "CF"], w=[SMALL])
            yield
            mm(ps[0][:, 16:24], cf(2), sm(2), True, True, r=["CF", SMALL], w=[PSN[0]])
            for c in range(2):
                mm(ps[c][:, 32:40], CF[c * 64:(c + 1) * 64, 1, :], small[c * 64:(c + 1) * 64, 2, :], True, True,
                   r=["CF", SMALL], w=[PSN[c]])
            cp('dve', sm(4), ps[0][:, 16:24], r=[PSN[0]], w=[SMALL])
            cp('dve', sm(5), ps[0][:, 32:40], r=[PSN[0]], w=[SMALL])
            cp('dve', sm(6), ps[1][:, 32:40], r=[PSN[1]], w=[SMALL])
            yield
            act(sm(7), sm(4), AF.Exp, r=[SMALL], w=[SMALL])
            act(small[:, 8:10, :], small[:, 5:7, :], AF.Exp, r=[SMALL], w=[SMALL])
            for c in range(2):
                rr = slice(c * 64, (c + 1) * 64)
                tt('dve', small[rr, 10, :], small[rr, 5 + c, :], small[rr, 4, :], ALU.subtract, r=[SMALL], w=[SMALL])
            act(sm(10), sm(10), AF.Exp, r=[SMALL], w=[SMALL])
            stt('dve', sm(11), sm(3), -1.0, sm(7), ALU.mult, ALU.mult, r=[SMALL], w=[SMALL])
            yield
            S.op('pool', lambda e: e.tensor_copy(xb[:, :, 0:3], xb[:, :, 128:131]), r=["xb"], w=["xb"])
            for g3 in range(3):
                for c in range(4):
                    for kc in range(8):
                        mm(ps[1 + g3][:, c * 128:(c + 1) * 128],
                           Win[:, kc, C_QKVB + g3 * 512 + c * 128:C_QKVB + g3 * 512 + (c + 1) * 128], hT[:, kc, :],
                           kc == 0, kc == 7, r=["Win", hTn], w=[PSN[1 + g3]])
                    yield
                cp('act' if g3 % 2 else 'dve', xb[:, g3 * 4:(g3 + 1) * 4, 3:131], v4(ps[1 + g3][:]), r=[PSN[1 + g3]], w=["xb"])
                yield
            proj_tm(C_ZB, 512, 0)
            act(zb_s[:], ps[0][:], AF.Silu, r=[PSN[0]], w=["zb_s"])
            yield
            if sample or t == NT - 1:
                lo = 32 if sample else 128
                dst = ncs[l, t] if sample else ncp[l]
                nct, rnct = cs.rearrange("p a b -> p (a b)"), rcs
                for c in range(12):
                    tr(ps[1 + c // 4][0:3, (c % 4) * 128:(c % 4 + 1) * 128], xb[:, c, lo:lo + 3], cf(0), r=["xb", "CF"], w=[PSN[1 + c // 4]])
                for g3 in range(3):
                    cp('dve', nct[0:3, g3 * 512:(g3 + 1) * 512], ps[1 + g3][0:3, :], r=[PSN[1 + g3]], w=rnct)
                dma(dst, nct[0:3, :], r=rnct, w=["ncout"], eng='sp')
                yield
            cgv = slice(8, 12)
            for i in range(4):
                wbc = cw[:, cgv, i:i + 1].to_broadcast([128, 4, 128])
                if i == 0:
                    tt('pool', cs[:, cgv, :], xb[:, cgv, 0:128], wbc, ALU.mult, r=["xb", "cw"], w=rcs_v)
                else:
                    tt('pool', ctv, xb[:, cgv, i:i + 128], wbc, ALU.mult, r=["xb", "cw"], w=rctv)
                    tt('pool', cs[:, cgv, :], cs[:, cgv, :], ctv, ALU.add, r=[rcs_v, rctv], w=rcs_v)
            cgq = slice(0, 8)
            for i in range(4):
                wbc = cw[:, cgq, i:i + 1].to_broadcast([128, 8, 128])
                if i == 0:
                    tt('dve', cs[:, cgq, :], xb[:, cgq, 0:128], wbc, ALU.mult, r=["xb", "cw"], w=rcs_qk)
                else:
                    tt('dve', ctmp, xb[:, cgq, i:i + 128], wbc, ALU.mult, r=["xb", "cw"], w=rct)
                    tt('dve', cs[:, cgq, :], cs[:, cgq, :], ctmp, ALU.add, r=[rcs_qk, rct], w=rcs_qk)
                yield
            act(cs, cs, AF.Silu, r=rcs, w=rcs)
            act(sq, cs[:, 0:8, :], AF.Square, r=rcs, w=rsq)
            yield
            for c in range(8):
                mm(ps[c // 4][:, (c % 4) * 128:(c % 4 + 1) * 128], bdonesb[:], sq[:, c, :], True, True,
                   r=["bdonesb", rsq], w=[PSN[c // 4]])
            yield
            for g2 in range(2):
                act(rinv[:, g2 * 4:(g2 + 1) * 4, :], v4(ps[g2][:]), AF.Ln, r=[PSN[g2]], w=rri, bias=NORM_EPS, scale=1.0)
            act(rinv, rinv, AF.Exp, r=rri, w=rri, scale=-0.5)
            yield
            stt('dve', qh, cs[:, 0:4, :], 0.125, rinv[:, 0:4, :], ALU.mult, ALU.mult, r=[rcs, rri], w=rqh)
            tt('dve', khf, cs[:, 4:8, :], rinv[:, 4:8, :], ALU.mult, r=[rcs, rri], w=rkhf)
            cp('act', kh, khf, r=rkhf, w=rkh)
            yield
            for c in range(4):
                tr(ps[2][:, c * 128:(c + 1) * 128], khf[:, c, :], cf(0), r=[rkhf, "CF"], w=[PSN[2]])
                tr(ps[3][:, c * 128:(c + 1) * 128], cs[:, 8 + c, :], cf(0), r=[rcs, "CF"], w=[PSN[3]])
            yield
            tt('dve', v8(kdec), v8(ps[2][:]), sm(10).unsqueeze(2).to_broadcast([128, 8, 64]), ALU.mult, r=[PSN[2], SMALL], w=rkd)
            tt('dve', v8(vbeta), v8(ps[3][:]), sm(3).unsqueeze(2).to_broadcast([128, 8, 64]), ALU.mult, r=[PSN[3], SMALL], w=rvb)
            yield

        merge([g_attn(), g_pre()], list(cfg.get('W1', (1, 1))))
        AR.release(m_long)

        Ttb_g = [AR.get(256, BF16, (4, 128)) for _ in range(2)]
        QKT_g = [AR.get(256, BF16, (4, 128)) for _ in range(2)]
        m2 = AR.mark()
        PB = []
        for par in range(2):
            d_ = dict(heads=[par + 2 * a for a in range(4)], par=par)
            d_['b0'], d_['b1'], d_['b2'] = (2, 3, 4) if par == 0 else (5, 6, 7)
            d_['Bm'], d_['rB'] = AR.get(512, F32, (4, 128))
            d_['dec'], d_['rdec'] = AR.get(512, F32, (4, 128))
            d_['Nf'], d_['rNf'] = AR.get(512, F32, (4, 128))
            d_['N'], d_['rN'] = AR.get(256, BF16, (4, 128))
            d_['Nt'], d_['rNt'] = AR.get(256, BF16, (4, 128))
            d_['Ttb'], d_['rTb'] = Ttb_g[par]
            d_['QKT'], d_['rQ'] = QKT_g[par]
            PB.append(d_)
        def g_w():
            for q in PB:
                g_bc = small[:, 2, :].rearrange("p (a b) -> p a b", b=2)[:, :, q['par']].unsqueeze(2).to_broadcast([128, 4, 128])
                tt('dve', q['Bm'], cf(2).unsqueeze(1).to_broadcast([128, 4, 128]), g_bc, ALU.mult, r=["CF", SMALL], w=q['rB'])
            yield
            for q in PB:
                b0 = q['b0']
                for a, hh in enumerate(q['heads']):
                    outp = ps[b0][:, a * 128:(a + 1) * 128]
                    mm(outp, cf(1), q['Bm'][:, a, :], True, False, r=["CF", q['rB']], w=[PSN[b0]])
                    mm(outp, identb[:], MBb[:], False, True, r=["identb", "MBb"], w=[PSN[b0]])
            yield
            for q in PB:
                b0 = q['b0']
                for a, hh in enumerate(q['heads']):
                    act(q['dec'][:, a, :], ps[b0][:, a * 128:(a + 1) * 128], AF.Exp, r=[PSN[b0], SMALL], w=q['rdec'],
                        bias=small[:, 4, hh:hh + 1], scale=-1.0)
            yield
            for q in PB:
                b1, par = q['b1'], q['par']
                for a, hh in enumerate(q['heads']):
                    pr = hh // 2
                    mm(ps[b1][:, a * 128:(a + 1) * 128], kh[par * 64:(par + 1) * 64, pr, :], kh[par * 64:(par + 1) * 64, pr, :],
                       True, True, r=[rkh], w=[PSN[b1]])
            yield
            for q in PB:
                b1, par = q['b1'], q['par']
                tt('dve', q['Bm'], v4(ps[b1][:]), q['dec'], ALU.mult, r=[PSN[b1], q['rdec']], w=q['rB'])
            yield
            for q in PB:
                par = q['par']
                beta_bc = small[:, 3, :].rearrange("p (a b) -> p a b", b=2)[:, :, par].unsqueeze(2).to_broadcast([128, 4, 128])
                tt('dve', q['Bm'], q['Bm'], beta_bc, ALU.mult, r=[q['rB'], SMALL], w=q['rB'])
                tt('dve', q['Nf'], q['Bm'], cf(4).unsqueeze(1).to_broadcast([128, 4, 128]), ALU.mult, r=[q['rB'], "CF"], w=q['rNf'])
            yield
            for q in PB:
                b0 = q['b0']
                cp('act', q['N'], q['Nf'], r=q['rNf'], w=q['rN'])
                for a in range(4):
                    tr(ps[b0][:, a * 128:(a + 1) * 128], q['Nf'][:, a, :], cf(0), r=[q['rNf'], "CF"], w=[PSN[b0]])
            yield
            for q in PB:
                b0 = q['b0']
                cp('act', q['Nt'], v4(ps[b0][:]), r=[PSN[b0]], w=q['rNt'])
                tt('dve', q['Nf'], v4(ps[b0][:]), cf(0).unsqueeze(1).to_broadcast([128, 4, 128]), ALU.add, r=[PSN[b0], "CF"], w=q['rNf'])
            yield
            for q in PB:
                cp('act', q['Ttb'], q['Nf'], r=q['rNf'], w=q['rTb'])
            yield
            for lev in range(5):
                for q in PB:
                    b0, b1 = q['b0'], q['b1']
                    for a in range(4):
                        mm(ps[b0][:, a * 128:(a + 1) * 128], q['Nt'][:, a, :], q['N'][:, a, :], True, True, r=[q['rNt'], q['rN']], w=[PSN[b0]])
                    if lev < 4:
                        for a in range(4):
                            mm(ps[b1][:, a * 128:(a + 1) * 128], q['N'][:, a, :], q['Nt'][:, a, :], True, True, r=[q['rNt'], q['rN']], w=[PSN[b1]])
                yield
                for q in PB:
                    b0, b1 = q['b0'], q['b1']
                    cp('act', q['N'], v4(ps[b0][:]), r=[PSN[b0]], w=q['rN'])
                    if lev < 4:
                        cp('dve', q['Nt'], v4(ps[b1][:]), r=[PSN[b1]], w=q['rNt'])
                yield
                for q in PB:
                    b2 = q['b2']
                    for a in range(4):
                        mm(ps[b2][:, a * 128:(a + 1) * 128], q['N'][:, a, :], q['Ttb'][:, a, :], True, True, r=[q['rN'], q['rTb']], w=[PSN[b2]])
                yield
                for q in PB:
                    b2 = q['b2']
                    tt('dve', q['Nf'], q['Nf'], v4(ps[b2][:]), ALU.add, r=[q['rNf'], PSN[b2]], w=q['rNf'])
                    cp('act', q['Ttb'], q['Nf'], r=q['rNf'], w=q['rTb'])
            yield
            for q in PB:
                b0, par = q['b0'], q['par']
                for a, hh in enumerate(q['heads']):
                    pr = hh // 2
                    mm(ps[b0][:, a * 128:(a + 1) * 128], qh[par * 64:(par + 1) * 64, pr, :], kh[par * 64:(par + 1) * 64, pr, :],
                       True, True, r=[rqh, rkh], w=[PSN[b0]])
            yield
            for q in PB:
                tt('dve', q['Bm'], v4(ps[q['b0']][:]), q['dec'], ALU.mult, r=[PSN[q['b0']], q['rdec']], w=q['rB'])
            yield
            for q in PB:
                b1 = q['b1']
                for a in range(4):
                    tr(ps[b1][:, a * 128:(a + 1) * 128], q['Bm'][:, a, :], cf(0), r=[q['rB'], "CF"], w=[PSN[b1]])
            yield
            for q in PB:
                cp('act', q['QKT'], v4(ps[q['b1']][:]), r=[PSN[q['b1']]], w=q['rQ'])

            yield

        if next_ac is not None:
            merge([g_w(), next_ac], [1, 1])
        else:
            for _ in g_w():
                pass
        AR.release(m2)

        Zb, rZ = AR.get(256, BF16)
        vn, rvn = AR.get(256, BF16)
        qSe, rqs = AR.get(512)
        og, rog = AR.get(512)
        ztmp, rzt = og, rog
        mix, rmix = AR.top(1024)
        sgb, rsgb = AR.top(1024)
        sg, rsg = AR.top(512)

        def g_rec():
            for c in range(2):
                rr = slice(c * 64, (c + 1) * 64)
                bk = 4 + c * 2
                for pr in range(4):
                    mm(ps[bk][rr, pr * 128:(pr + 1) * 128], kh[:, pr, c * 64:(c + 1) * 64], Sb[:, pr, :], True, True,
                       r=[rkh, "Sb"], w=[PSN[bk]])
                for pr in range(4):
                    mm(ps[bk + 1][rr, pr * 128:(pr + 1) * 128], qh[:, pr, c * 64:(c + 1) * 64], Sb[:, pr, :], True, True,
                       r=[rqh, "Sb"], w=[PSN[bk + 1]])
                yield
                tt('dve', v8(ztmp[rr, :]), v8(ps[bk][rr, :]), small[rr, 11, :].unsqueeze(2).to_broadcast([64, 8, 64]), ALU.mult,
                   r=[PSN[bk], SMALL], w=rzt)
                tt('dve', Zb[rr, :], ztmp[rr, :], vbeta[rr, :], ALU.add, r=[rzt, rvb], w=rZ)
                tt('dve', v8(qSe[rr, :]), v8(ps[bk + 1][rr, :]), small[rr, 7, :].unsqueeze(2).to_broadcast([64, 8, 64]), ALU.mult,
                   r=[PSN[bk + 1], SMALL], w=rqs)
                yield
                for hh in range(8):
                    par, a = hh % 2, hh // 2
                    Ttb, rTb = Ttb_g[par]
                    mm(ps[bk][rr, hh * 64:(hh + 1) * 64], Ttb[rr, a, c * 64:(c + 1) * 64], Zb[rr, hh * 64:(hh + 1) * 64], True, True,
                       r=[rTb, rZ], w=[PSN[bk]])
                yield
                cp('act', vn[rr, :], ps[bk][rr, :], r=[PSN[bk]], w=rvn)
                yield
                for hh in range(8):
                    par, a = hh % 2, hh // 2
                    QKT, rQ = QKT_g[par]
                    mm(ps[bk + 1][rr, hh * 64:(hh + 1) * 64], QKT[rr, a, c * 64:(c + 1) * 64], vn[rr, hh * 64:(hh + 1) * 64], True, True,
                       r=[rQ, rvn], w=[PSN[bk + 1]])
                for pr in range(4):
                    mm(ps[bk][:, pr * 128:(pr + 1) * 128], kdec[rr, pr * 128:(pr + 1) * 128], vn[rr, pr * 128:(pr + 1) * 128], True, True,
                       r=[rkd, rvn], w=[PSN[bk]])
                yield
                S.op('dve', lambda e, c=c: e.tensor_tensor(
                    Sbd[:].rearrange("p a (b d) -> p a b d", b=2), Sbd[:].rearrange("p a (b d) -> p a b d", b=2),
                    small[:, 8 + c, :].rearrange("p (a b) -> p a b", b=2).unsqueeze(3).to_broadcast([128, 4, 2, 64]), ALU.mult),
                    r=["Sbd", SMALL], w=["Sbd"])
                for par in range(2):
                    pp = slice(par * 64, (par + 1) * 64)
                    tt('dve', Sbd[pp, :, par * 64:(par + 1) * 64], Sbd[pp, :, par * 64:(par + 1) * 64],
                       v4(ps[bk][pp, :])[:, :, par * 64:(par + 1) * 64], ALU.add, r=["Sbd", PSN[bk]], w=["Sbd"])
                cp('act', Sb[:], Sbd[:], r=["Sbd"], w=["Sb"])
                yield
                tt('dve', og[rr, :], ps[bk + 1][rr, :], qSe[rr, :], ALU.add, r=[PSN[bk + 1], rqs], w=rog)
                yield
            if sample or t == NT - 1:
                dst = ngs[l, t] if sample else ngp[l]
                for pr in range(4):
                    for par in range(2):
                        dma(dst[pr * 2 + par], Sbd[par * 64:(par + 1) * 64, pr, par * 64:(par + 1) * 64], r=["Sbd"], w=["ngout"], eng='sp')
            osq, ros = qSe, rqs
            act(osq, og, AF.Square, r=rog, w=ros)
            S.op('dve', lambda e: e.tensor_reduce(small[:, 12, :], v8(osq), AX.X, ALU.add), r=ros, w=[SMALL])
            act(sm(12), sm(12), AF.Ln, r=[SMALL], w=[SMALL], bias=NORM_EPS, scale=1.0 / 64.0)
            act(sm(12), sm(12), AF.Exp, r=[SMALL], w=[SMALL], scale=-0.5)
            yield
            tt('dve', v8(osq), v8(og), sm(12).unsqueeze(2).to_broadcast([128, 8, 64]), ALU.mult, r=[rog, SMALL], w=ros)
            tt('dve', v8(osq), v8(osq), gnws[:].unsqueeze(1).to_broadcast([128, 8, 64]), ALU.mult, r=[ros, "gnws"], w=ros)
            tt('dve', osq, osq, zb_s[:], ALU.mult, r=[ros, "zb_s"], w=ros)
            yield
            for c in range(4):
                tr(ps[4][:, c * 128:(c + 1) * 128], osq[:, c * 128:(c + 1) * 128], cf(0), r=[ros, "CF"], w=[PSN[4]])
            cp('act', obgT[:], v4(ps[4][:]), r=[PSN[4]], w=["obgT"])
            yield

        def g_early():
            for nb in range(2):
                for kc in range(4):
                    mm(ps[0][:], oagT[:, kc, :], Wa[:, kc, nb * 512:(nb + 1) * 512], kc == 0, kc == 3, r=["oagT", "Wa"], w=[PSN[0]])
                yield
                for kc in range(8):
                    mm(ps[2][:], hT[:, kc, :], Win[:, kc, C_GA + nb * 512:C_GA + (nb + 1) * 512], kc == 0, kc == 7,
                       r=["Win", hTn], w=[PSN[2]])
                    if kc == 3:
                        yield
                yield
                for kc in range(8):
                    mm(ps[3][:], hT[:, kc, :], Win[:, kc, C_GB + nb * 512:C_GB + (nb + 1) * 512], kc == 0, kc == 7,
                       r=["Win", hTn], w=[PSN[3]])
                    if kc == 3:
                        yield
                yield
                act(sg, ps[2][:], AF.Tanh, r=[PSN[2]], w=rsg, scale=0.5)
                stt('dve', mix[:, nb * 512:(nb + 1) * 512], sg, 1.0, ps[0][:], ALU.add, ALU.mult, r=[PSN[0], rsg], w=rmix)
                act(sgb[:, nb * 512:(nb + 1) * 512], ps[3][:], AF.Tanh, r=[PSN[3]], w=rsgb, scale=0.5)
                yield

        merge([g_rec(), g_early()], list(cfg.get('W2', (1, 1))))
        AR.reset()

        mtmp, rmt = AR.get(512)
        for nb in range(2):
            for kc in range(4):
                mm(ps[1][:], obgT[:, kc, :], Wb[:, kc, nb * 512:(nb + 1) * 512], kc == 0, kc == 3, r=["obgT", "Wb"], w=[PSN[1]])
            stt('dve', mtmp, sgb[:, nb * 512:(nb + 1) * 512], 1.0, ps[1][:], ALU.add, ALU.mult, r=[PSN[1], rsgb], w=rmt)
            tt('dve', mix[:, nb * 512:(nb + 1) * 512], mix[:, nb * 512:(nb + 1) * 512], mtmp, ALU.add, r=[rmix, rmt], w=rmix)
        mixT, rmT = AR.get(512, BF16, (8, 128))
        for half in range(2):
            for c in range(4):
                kc = half * 4 + c
                tr(ps[4 + half][:, c * 128:(c + 1) * 128], mix[:, kc * 128:(kc + 1) * 128], cf(0), r=[rmix, "CF"], w=[PSN[4 + half]])
            if half:
                act(mixT[:, 4:8, :], v4(ps[5][:]), AF.Copy, r=[PSN[5]], w=rmT, scale=0.5)
            else:
                ts('dve', mixT[:, 0:4, :], v4(ps[4][:]), 0.5, None, ALU.mult, None, r=[PSN[4]], w=rmT)
        z, rz = AR.get(1024)
        for nb in range(2):
            for kc in range(8):
                mm(ps[6 + nb][:], mixT[:, kc, :], Wo[:, kc, nb * 512:(nb + 1) * 512], kc == 0, kc == 7, r=[rmT, "Wo"], w=[PSN[6 + nb]])
            stt('dve', z[:, nb * 512:(nb + 1) * 512], h[:, nb * 512:(nb + 1) * 512], ALPHA, ps[6 + nb][:], ALU.mult, ALU.add,
                r=[hn, PSN[6 + nb]], w=rz)
        layer_norm_rows(z, z, rz, rz)
        dma(hout[t * 128:(t + 1) * 128, :], z, r=rz, w=[hout_name], eng='sp')

    def merge(gens, weights):
        live = list(gens)
        while live:
            for g_, w_ in list(zip(live, weights)):
                for _ in range(w_):
                    try:
                        next(g_)
                    except StopIteration:
                        idx = live.index(g_)
                        live.pop(idx)
                        weights = weights[:idx] + weights[idx + 1:]
                        break

    tix = 0
    STOP = cfg.get('STOP')
    for l in range(DEPTH):
        if STOP == 'prologue':
            break
        if not cfg.get('NOLOAD'):
            load_layer(l)
        if STOP == 'load':
            break
        seq = [('p', t) for t in range(NT)] + [('s', t) for t in range(NSB)]
        issue_load(l, seq[0][0], seq[0][1], tix)
        ac_done = False
        for i, (kind, t) in enumerate(seq):
            pf = None
            nac = None
            if i + 1 < len(seq):
                k2, t2 = seq[i + 1]
                pf = (lambda k2=k2, t2=t2, tx=tix + 1, l=l: issue_load(l, k2, t2, tx))
                if k2 == 'p' and 1 <= t2 < NT - NKT and not cfg.get('NOOVL'):
                    nac = ac_stage(l, k2, t2, tix + 1)
            tile(l, kind, t, tix, ac_done=ac_done, prefetch=pf, next_ac=nac)
            ac_done = nac is not None
            tix += 1

    S.emit(nc, es)
    es.close()
    return nc


_IDX = bias_index()


def make_in_maps(cfg, inputs):
    T, NSB, DEPTH, NCORES = cfg['T'], cfg['NSB'], cfg['DEPTH'], cfg['NCORES']
    f = lambda a: np.ascontiguousarray(np.asarray(a, dtype=np.float32))
    x_prompt = f(inputs['x_prompt'])
    x_sample = f(inputs['x_sample'])
    NB = x_prompt.shape[0]
    bc = lambda v, n=128: np.ascontiguousarray(np.broadcast_to(np.asarray(v, np.float32)[None, :], (n, v.shape[-1])))
    rel = f(inputs['rel_bias'])
    btf = np.stack([np.stack([np.stack([rel[l, h][_IDX[pi]] for pi in range(3)]) for h in range(8)]) for l in range(DEPTH)])
    crow = np.stack([np.broadcast_to(rel[l, :, 191][None, :], (128, 8)) for l in range(DEPTH)])
    shared = dict(
        w_in=f(inputs['w_in']), w_a=f(inputs['w_branch_a']), w_b=f(inputs['w_branch_b']), w_o=f(inputs['w_out']),
        convw=np.ascontiguousarray(f(inputs['conv_w']).transpose(0, 2, 1)),
        ln0=np.stack([bc(f(inputs['ln0_g'])), bc(f(inputs['ln0_b']))]),
        lnl=np.stack([np.stack([bc(f(inputs['ln_g'])[l]), bc(f(inputs['ln_b'])[l])]) for l in range(DEPTH)]),
        alog=np.stack([bc(f(inputs['a_log'])[l]) for l in range(DEPTH)]),
        dtb=np.stack([bc(f(inputs['dt_bias'])[l]) for l in range(DEPTH)]),
        gnw=np.stack([bc(f(inputs['gdn_norm_w'])[l]) for l in range(DEPTH)]),
        bt=f(btf), crow=f(crow), consts=host_consts(),
    )
    in_maps = []
    for c in range(NCORES):
        b = c % NB
        sbs = list(range(c * NSB, (c + 1) * NSB))
        xs = np.zeros((NSB, 128, D), np.float32)
        xs[:, 0:32, :] = x_sample[sbs]
        m = dict(shared)
        m['xp'] = x_prompt[b]
        m['xs'] = xs.reshape(NSB * 128, D)
        m['ck'] = f(inputs['cache_attn_k'][:, sbs]).reshape(DEPTH, NSB, 512, 512)
        m['cv'] = f(inputs['cache_attn_v'][:, sbs]).reshape(DEPTH, NSB, 512, 512)
        m['sconv'] = f(inputs['state_conv'][:, sbs])
        m['sgdn'] = f(inputs['state_gdn'][:, sbs])
        in_maps.append(m)
    return in_maps


def gather(cfg, res, NB):
    T, NSB, DEPTH, NCORES = cfg['T'], cfg['NSB'], cfg['DEPTH'], cfg['NCORES']
    KEEP = min(512, T)
    R = res
    yp = np.stack([R[b]['yp'] for b in range(NB)])
    ys = np.concatenate([R[c]['ys'].reshape(NSB, 128, D)[:, 0:32] for c in range(NCORES)], axis=0)
    nkp = np.stack([R[b]['nkp'] for b in range(NB)], axis=1).reshape(DEPTH, NB, KEEP, 8, 64)
    nvp = np.stack([R[b]['nvp'] for b in range(NB)], axis=1).reshape(DEPTH, NB, KEEP, 8, 64)
    ncp = np.stack([R[b]['ncp'] for b in range(NB)], axis=1)
    ngp = np.stack([R[b]['ngp'] for b in range(NB)], axis=1)
    nks = np.concatenate([R[c]['nks'] for c in range(NCORES)], axis=1).reshape(DEPTH, NCORES * NSB, 32, 8, 64)
    nvs = np.concatenate([R[c]['nvs'] for c in range(NCORES)], axis=1).reshape(DEPTH, NCORES * NSB, 32, 8, 64)
    ncs = np.concatenate([R[c]['ncs'] for c in range(NCORES)], axis=1)
    ngs = np.concatenate([R[c]['ngs'] for c in range(NCORES)], axis=1)
    return tuple(np.ascontiguousarray(a, dtype=np.float32) for a in (yp, ys, nkp, nvp, ncp, ngp, nks, nvs, ncs, ngs))


_NC_CACHE = {}


def kernel(**inputs):
    cfg = dict(CFG)
    key = tuple(sorted(cfg.items()))
    if key not in _NC_CACHE:
        _NC_CACHE[key] = build(cfg)
    nc = _NC_CACHE[key]
    in_maps = make_in_maps(cfg, inputs)
    res = run_bass_kernel_spmd(nc, in_maps, core_ids=list(range(cfg['NCORES'])))
    return gather(cfg, res.results, np.asarray(inputs['x_prompt']).shape[0])
```
